# Optimizing a Trainium2 kernel written in Bass

```python
import jax
import jax.numpy as jnp
from jax import lax
import numpy as np

D_MODEL = 1024
BATCH = 8
SEQ = 8192
DEPTH = 1

CHUNK = 64
Q_BLOCK = 128
HEAD_DIM = 64
FOX_HEADS = 8
DSA_HEADS = 8
IDX_HEADS = 8
IDX_DIM = 32
TOPK_MAX = 256
ROPE_THETA = 500000.0
ROPE_FRACTION = 4
D_FF = 2816
N_SUBLAYERS = 3
NORM_EPS = 1e-6
FOX_WIDTH = FOX_HEADS * HEAD_DIM
DSA_WIDTH = DSA_HEADS * HEAD_DIM
IN_SIZES = (FOX_WIDTH, FOX_WIDTH, FOX_WIDTH, FOX_HEADS, DSA_WIDTH, HEAD_DIM, HEAD_DIM, IDX_HEADS * IDX_DIM, IDX_DIM, IDX_HEADS, D_MODEL, D_MODEL)
D_IN = sum(IN_SIZES)

kernel_name = 'hybrid_fox_dsa_macaron_block'


def rms_norm(x, g):
    xf = x.astype(jnp.float32)
    y = xf * lax.rsqrt(jnp.mean(xf * xf, axis=-1, keepdims=True) + NORM_EPS)
    return (y * g.astype(jnp.float32)).astype(x.dtype)


def modulate(x, g, shift, scale):
    return rms_norm(x, g) * (1 + scale[:, None, :]) + shift[:, None, :]


def swiglu(h, w1, w3, w2):
    return (jax.nn.silu(h @ w1) * (h @ w3)) @ w2


def rope_tables(positions, rot_dim):
    inv_freq = ROPE_THETA ** (-jnp.arange(0, rot_dim, 2, dtype=jnp.float32) / rot_dim)
    ang = positions.astype(jnp.float32)[..., None] * inv_freq
    return jnp.cos(ang), jnp.sin(ang)


def partial_rope(x, cos, sin):
    half = cos.shape[-1]
    x1 = x[..., :half].astype(jnp.float32)
    x2 = x[..., half:2 * half].astype(jnp.float32)
    r1 = (x1 * cos - x2 * sin).astype(x.dtype)
    r2 = (x2 * cos + x1 * sin).astype(x.dtype)
    return jnp.concatenate([r1, r2, x[..., 2 * half:]], axis=-1)


def fox_attention(q, k, v, log_f):
    B, S, H, dh = q.shape
    F = jnp.cumsum(log_f.astype(jnp.float32), axis=1).transpose(0, 2, 1)
    kpos = jnp.arange(S)
    scale = dh ** -0.5

    def block(i):
        qs = i * Q_BLOCK
        qb = lax.dynamic_slice_in_dim(q, qs, Q_BLOCK, axis=1)
        Fq = lax.dynamic_slice_in_dim(F, qs, Q_BLOCK, axis=2)
        s = jnp.einsum('bqhd,bkhd->bhqk', qb, k, preferred_element_type=jnp.float32) * scale
        s = s + Fq[..., None] - F[:, :, None, :]
        qpos = qs + jnp.arange(Q_BLOCK)
        mask = kpos[None, :] <= qpos[:, None]
        s = jnp.where(mask[None, None], s, -jnp.inf)
        p = jax.nn.softmax(s, axis=-1)
        return jnp.einsum('bhqk,bkhd->bqhd', p.astype(v.dtype), v)

    out = lax.map(block, jnp.arange(S // Q_BLOCK))
    return out.transpose(1, 0, 2, 3, 4).reshape(B, S, H, dh)


def dsa_attention(q, k, v, iq, ik, iw, top_k):
    B, S, H, dh = q.shape
    kchunk = jnp.arange(S) // CHUNK
    scale = dh ** -0.5
    idx_scale = IDX_DIM ** -0.5

    def block(i):
        qs = i * Q_BLOCK
        qb = lax.dynamic_slice_in_dim(q, qs, Q_BLOCK, axis=1)
        iqb = lax.dynamic_slice_in_dim(iq, qs, Q_BLOCK, axis=1)
        iwb = lax.dynamic_slice_in_dim(iw, qs, Q_BLOCK, axis=1)
        qchunk = (qs + jnp.arange(Q_BLOCK)) // CHUNK
        admissible = kchunk[None, :] <= qchunk[:, None]
        dots = jnp.einsum('bqhd,bkd->bqhk', iqb, ik, preferred_element_type=jnp.float32) * idx_scale
        score = jnp.einsum('bqh,bqhk->bqk', iwb.astype(jnp.float32), jax.nn.relu(dots))
        score = jnp.where(admissible[None], score, -jnp.inf)
        _, sel = lax.top_k(score, top_k)
        kg = jax.vmap(lambda kb, ib: kb[ib])(k, sel)
        vg = jax.vmap(lambda vb, ib: vb[ib])(v, sel)
        valid = (sel // CHUNK) <= qchunk[None, :, None]
        s = jnp.einsum('bqhd,bqkd->bhqk', qb, kg, preferred_element_type=jnp.float32) * scale
        s = jnp.where(valid[:, None], s, -jnp.inf)
        p = jax.nn.softmax(s, axis=-1)
        return jnp.einsum('bhqk,bqkd->bqhd', p.astype(vg.dtype), vg)

    out = lax.map(block, jnp.arange(S // Q_BLOCK))
    return out.transpose(1, 0, 2, 3, 4).reshape(B, S, H, dh)


def hybrid_mixer(h, w_in, fox_f_bias, fox_qk_g, dsa_qk_g, w_br_fox, w_br_dsa, w_out, rope_a, rope_i, top_k):
    B, S, _ = h.shape
    cos_a, sin_a = rope_a
    cos_i, sin_i = rope_i
    z = h @ w_in
    split_points = [int(p) for p in np.cumsum(IN_SIZES)[:-1]]
    (fq, fk, fv, ff, dq, dk, dv, iq, ik, iw, ga, gb) = jnp.split(z, split_points, axis=-1)
    fq = rms_norm(fq.reshape(B, S, FOX_HEADS, HEAD_DIM), fox_qk_g[0])
    fk = rms_norm(fk.reshape(B, S, FOX_HEADS, HEAD_DIM), fox_qk_g[1])
    fv = fv.reshape(B, S, FOX_HEADS, HEAD_DIM)
    log_f = jax.nn.log_sigmoid(ff.astype(jnp.float32) + fox_f_bias.astype(jnp.float32))
    ya = fox_attention(fq, fk, fv, log_f).reshape(B, S, FOX_WIDTH)
    dq = partial_rope(rms_norm(dq.reshape(B, S, DSA_HEADS, HEAD_DIM), dsa_qk_g[0]), cos_a[:, :, None, :], sin_a[:, :, None, :])
    dk = partial_rope(rms_norm(dk, dsa_qk_g[1]), cos_a, sin_a)
    iq = partial_rope(iq.reshape(B, S, IDX_HEADS, IDX_DIM), cos_i[:, :, None, :], sin_i[:, :, None, :])
    ik = partial_rope(ik, cos_i, sin_i)
    iw = iw * (IDX_HEADS ** -0.5)
    yb = dsa_attention(dq, dk, dv, iq, ik, iw, top_k).reshape(B, S, DSA_WIDTH)
    merged = jax.nn.sigmoid(ga) * (ya @ w_br_fox) + jax.nn.sigmoid(gb) * (yb @ w_br_dsa)
    return merged @ w_out


def setup_inputs(seed: int = 0) -> dict:
    key = jax.random.key(seed)
    ks = jax.random.split(key, 24)
    L = DEPTH

    def nrm(k, shape, fan_in, gain=1.0):
        return jax.random.normal(k, shape, jnp.float32) * (gain * fan_in ** -0.5)

    x = jax.random.normal(ks[0], (BATCH, SEQ, D_MODEL), jnp.float32)
    c = jax.random.normal(ks[1], (BATCH, D_MODEL), jnp.float32)
    offsets = jax.random.randint(ks[2], (BATCH, 1), 0, 1024) * CHUNK
    positions = (offsets + jnp.arange(SEQ, dtype=jnp.int32)[None, :]).astype(jnp.int32)
    ada_w = nrm(ks[3], (L, D_MODEL, 3 * N_SUBLAYERS * D_MODEL), D_MODEL, 0.5)
    ada_b = 0.02 * jax.random.normal(ks[4], (L, 3 * N_SUBLAYERS * D_MODEL), jnp.float32)
    norm_g = 1.0 + 0.05 * jax.random.normal(ks[5], (L, N_SUBLAYERS, D_MODEL), jnp.float32)
    ffn1_w1 = nrm(ks[6], (L, D_MODEL, D_FF), D_MODEL)
    ffn1_w3 = nrm(ks[7], (L, D_MODEL, D_FF), D_MODEL)
    ffn1_w2 = nrm(ks[8], (L, D_FF, D_MODEL), D_FF)
    w_in = nrm(ks[9], (L, D_MODEL, D_IN), D_MODEL)
    fox_f_bias = 2.0 + 0.1 * jax.random.normal(ks[10], (L, FOX_HEADS), jnp.float32)
    fox_qk_g = 1.0 + 0.05 * jax.random.normal(ks[11], (L, 2, HEAD_DIM), jnp.float32)
    dsa_qk_g = 1.0 + 0.05 * jax.random.normal(ks[12], (L, 2, HEAD_DIM), jnp.float32)
    w_br_fox = nrm(ks[13], (L, FOX_WIDTH, D_MODEL), FOX_WIDTH)
    w_br_dsa = nrm(ks[14], (L, DSA_WIDTH, D_MODEL), DSA_WIDTH)
    w_out = nrm(ks[15], (L, D_MODEL, D_MODEL), D_MODEL)
    ffn2_w1 = nrm(ks[16], (L, D_MODEL, D_FF), D_MODEL)
    ffn2_w3 = nrm(ks[17], (L, D_MODEL, D_FF), D_MODEL)
    ffn2_w2 = nrm(ks[18], (L, D_FF, D_MODEL), D_FF)
    return {'x': x, 'c': c, 'positions': positions, 'ada_w': ada_w, 'ada_b': ada_b,
            'norm_g': norm_g, 'ffn1_w1': ffn1_w1, 'ffn1_w3': ffn1_w3, 'ffn1_w2': ffn1_w2,
            'w_in': w_in, 'fox_f_bias': fox_f_bias, 'fox_qk_g': fox_qk_g, 'dsa_qk_g': dsa_qk_g,
            'w_br_fox': w_br_fox, 'w_br_dsa': w_br_dsa, 'w_out': w_out,
            'ffn2_w1': ffn2_w1, 'ffn2_w3': ffn2_w3, 'ffn2_w2': ffn2_w2}


def reference(x, c, positions, ada_w, ada_b, norm_g, ffn1_w1, ffn1_w3, ffn1_w2, w_in, fox_f_bias, fox_qk_g, dsa_qk_g, w_br_fox, w_br_dsa, w_out, ffn2_w1, ffn2_w3, ffn2_w2):
    B, S, _ = x.shape
    top_k = min(TOPK_MAX, S // 4)
    rope_a = rope_tables(positions, HEAD_DIM // ROPE_FRACTION)
    rope_i = rope_tables(positions, IDX_DIM // ROPE_FRACTION)
    cond = jax.nn.silu(c)
    for l in range(DEPTH):
        mod = (cond @ ada_w[l] + ada_b[l]).reshape(B, N_SUBLAYERS, 3, D_MODEL)
        h = modulate(x, norm_g[l, 0], mod[:, 0, 0], mod[:, 0, 1])
        x = x + 0.5 * mod[:, 0, 2][:, None, :] * swiglu(h, ffn1_w1[l], ffn1_w3[l], ffn1_w2[l])
        h = modulate(x, norm_g[l, 1], mod[:, 1, 0], mod[:, 1, 1])
        y = hybrid_mixer(h, w_in[l], fox_f_bias[l], fox_qk_g[l], dsa_qk_g[l], w_br_fox[l], w_br_dsa[l], w_out[l], rope_a, rope_i, top_k)
        x = x + mod[:, 1, 2][:, None, :] * y
        h = modulate(x, norm_g[l, 2], mod[:, 2, 0], mod[:, 2, 1])
        x = x + 0.5 * mod[:, 2, 2][:, None, :] * swiglu(h, ffn2_w1[l], ffn2_w3[l], ffn2_w2[l])
    return x
```

```python
import math
from contextlib import ExitStack
import numpy as np
import concourse.bass as bass
import concourse.mybir as mybir
from concourse.bass_utils import run_bass_kernel_spmd

F32 = mybir.dt.float32
BF16 = mybir.dt.bfloat16
I32 = mybir.dt.int32
AF = mybir.ActivationFunctionType
ALU = mybir.AluOpType
AX = mybir.AxisListType

D = 1024
DFF = 2816
NJ = 22
KC = 8
EPS = 1e-6
BIG = 1.0e30
NIT = 20
TOPK = 256
NH_DVE = 5
SB_BASE = 16640
SB_END = 229376
ENGS = ("pe", "act", "dve", "pool", "sp")


class Sched:
    def __init__(self, nc, es):
        self.nc = nc
        self.sem = {e: es.enter_context(nc.semaphore("s_" + e)) for e in ENGS}
        self.cnt = {e: 0 for e in ENGS}
        self.ops = {e: [] for e in ENGS}
        self.seen = {e: {} for e in ENGS}
        self.res = {}
        self.enabled = True
        self.dq = {}
        for q, n in (("sp", 24), ("pool", 16)):
            self.dq[q] = dict(sems=[es.enter_context(nc.semaphore("d_%s%d" % (q, i))) for i in range(n)],
                              tgt=[0] * n, nxt=0)

    def _wait(self, e, tok):
        if tok[0] == "e":
            _, e2, n = tok
            if e2 == e and e == "pe":
                return
            key = e2
        else:
            _, q, i, n = tok
            key = (q, i)
        if self.seen[e].get(key, 0) >= n:
            return
        self.seen[e][key] = n
        self.ops[e].append(("w", tok))

    def _deps(self, e, reads, writes):
        for r in reads:
            ent = self.res.get(r)
            if ent and ent[0]:
                self._wait(e, ent[0])
        for r in writes:
            ent = self.res.get(r)
            if ent:
                if ent[0]:
                    self._wait(e, ent[0])
                for t in ent[1]:
                    self._wait(e, t)

    def _commit(self, tok, reads, writes):
        src = tok[:2] if tok[0] == "e" else tok[:3]
        for r in writes:
            self.res[r] = [tok, []]
        for r in reads:
            ent = self.res.setdefault(r, [None, []])
            ent[1] = [t for t in ent[1] if (t[:2] if t[0] == "e" else t[:3]) != src] + [tok]

    def op(self, e, fn, reads=(), writes=()):
        if not self.enabled:
            return
        self._deps(e, reads, writes)
        self.cnt[e] += 1
        tok = ("e", e, self.cnt[e])
        self.ops[e].append(("o", fn))
        self._commit(tok, reads, writes)

    def dma(self, q, out, in_, reads=(), writes=()):
        if not self.enabled:
            return
        self._deps(q, reads, writes)
        d = self.dq[q]
        i = d["nxt"]
        d["nxt"] = (i + 1) % len(d["sems"])
        if d["tgt"][i] > 0:
            self._wait(q, ("d", q, i, d["tgt"][i]))
        d["tgt"][i] += 16
        tok = ("d", q, i, d["tgt"][i])
        self.ops[q].append(("d", out, in_, d["sems"][i]))
        self._commit(tok, reads, writes)

    def barrier(self):
        for e in ENGS:
            for e2 in ENGS:
                if self.cnt[e2] > 0:
                    self._wait(e, ("e", e2, self.cnt[e2]))
            for q, d in self.dq.items():
                for i, t in enumerate(d["tgt"]):
                    if t:
                        self._wait(e, ("d", q, i, t))
        self.res.clear()

    def emit(self, block):
        def run(e, h):
            for it in self.ops[e]:
                if it[0] == "w":
                    tok = it[1]
                    if tok[0] == "e":
                        h.wait_ge(self.sem[tok[1]], tok[2])
                    else:
                        h.wait_ge(self.dq[tok[1]]["sems"][tok[2]], tok[3])
                elif it[0] == "o":
                    it[1](h).then_inc(self.sem[e], 1)
                else:
                    h.dma_start(out=it[1], in_=it[2]).then_inc(it[3], 16)
        block.tensor(lambda h: run("pe", h))
        block.scalar(lambda h: run("act", h))
        block.vector(lambda h: run("dve", h))
        block.gpsimd(lambda h: run("pool", h))
        block.sync(lambda h: run("sp", h))

    def mm(self, out, lhsT, rhs, start, stop, r, w):
        self.op("pe", lambda h: h.matmul(out, lhsT, rhs, start=start, stop=stop), r, w)

    def tr(self, out, in_, ident, r, w):
        self.op("pe", lambda h: h.transpose(out, in_, ident), r, w)

    def act(self, out, in_, func, r, w, bias=0.0, scale=1.0):
        self.op("act", lambda h: h.activation(out, in_, func, bias=bias, scale=scale), r, w)

    def tt(self, e, out, a, b, op, r, w):
        self.op(e, lambda h: h.tensor_tensor(out, a, b, op), r, w)

    def ts(self, e, out, a, s1, s2, op0, op1, r, w, accum=None):
        if accum is None:
            self.op(e, lambda h: h.tensor_scalar(out, a, s1, s2, op0, op1), r, w)
        else:
            self.op(e, lambda h: h.tensor_scalar(out, a, s1, s2, op0, op1, accum_out=accum), r, w)

    def stt(self, e, out, in0, scalar, in1, op0, op1, r, w):
        self.op(e, lambda h: h.scalar_tensor_tensor(out, in0, scalar, in1, op0, op1), r, w)

    def cp(self, e, out, in_, r, w):
        if e == "act":
            self.op(e, lambda h: h.copy(out, in_), r, w)
        else:
            self.op(e, lambda h: h.tensor_copy(out, in_), r, w)

    def memset(self, e, ap, val, w):
        self.op(e, lambda h: h.memset(ap, val), (), w)

    def recip(self, out, in_, r, w):
        self.op("dve", lambda h: h.reciprocal(out, in_), r, w)


class Alloc:
    def __init__(self, nc):
        self.nc = nc
        self.off = SB_BASE
        self.n = 0

    def __call__(self, name, shape, dt):
        nbytes = int(np.prod(shape[1:])) * (4 if dt in (F32, I32) else 2)
        nbytes = (nbytes + 31) // 32 * 32
        assert self.off + nbytes <= SB_END, "SBUF overflow at %s: %d" % (name, self.off + nbytes)
        self.n += 1
        t = self.nc.alloc_sbuf_tensor_at("%s_%d" % (name, self.n), list(shape), dt, offset=self.off)
        self.off += nbytes
        return t

    def mark(self):
        return self.off

    def release(self, m):
        self.off = m


def _view(ap2d, shape):
    if len(shape) == 1:
        return ap2d
    if len(shape) == 2:
        return ap2d.rearrange("p (a b) -> p a b", a=shape[0])
    if len(shape) == 3:
        return ap2d.rearrange("p (a b c) -> p a b c", a=shape[0], b=shape[1])
    raise ValueError(shape)


def build(S, debug=False, phases="0AB1B2C"):
    NT = S // 128
    NB = S // 512
    nc = bass.Bass("TRN2", target_bir_lowering=False)
    es = ExitStack()

    def din(name, shape, dt=F32):
        return nc.dram_tensor(name, list(shape), dt, kind="ExternalInput").ap()

    def dscr(name, shape, dt):
        kind = "ExternalOutput" if (debug and name in DEBUG_OUT) else "Internal"
        return nc.dram_tensor(name, list(shape), dt, kind=kind).ap()

    DEBUG_OUT = ("X1T", "FQT", "FKT", "FV", "DQT", "DKT", "DV", "IQT", "IKT", "YAT", "YBT", "SGT", "DBG")

    xT = din("xT", [D, S])
    cT = din("cT", [128, KC])
    pos = din("pos", [128, NT], I32)
    ada_w = din("ada_w", [128, KC, 9 * D])
    ada_bT = din("ada_bT", [128, 72])
    norm_gT = din("norm_gT", [128, 3, KC])
    wf = {}
    for p in ("f1", "f2"):
        wf[p + "w1"] = din(p + "w1", [128, NJ * KC * 128])
        wf[p + "w3"] = din(p + "w3", [128, NJ * KC * 128])
        wf[p + "w2"] = din(p + "w2", [128, KC * NJ * 128])
    wf["wt"] = din("wt", [128, 5 * KC * 512])
    wf["wg"] = din("wg", [128, 16 * KC * 128])
    wf["wbf"] = din("wbf", [128, 4 * D])
    wf["wbd"] = din("wbd", [128, 4 * D])
    wf["wo"] = din("wo", [128, KC * D])
    fbias = din("fbias", [1, 8])
    fqg = din("fqg", [1, 128])
    dqg = din("dqg", [1, 128])
    outT = nc.dram_tensor("outT", [D, S], F32, kind="ExternalOutput").ap()

    wb = {k: dscr("b_" + k, list(v.shape), BF16) for k, v in wf.items()}
    X1T = dscr("X1T", [D, S], F32)
    SGT = dscr("SGT", [2 * D, S], BF16)
    FQT = dscr("FQT", [512, S], BF16)
    FKT = dscr("FKT", [512, S], BF16)
    FV = dscr("FV", [S, 512], BF16)
    DQT = dscr("DQT", [512, S], BF16)
    DKT = dscr("DKT", [64, S], BF16)
    DV = dscr("DV", [S, 64], BF16)
    IQT = dscr("IQT", [256, S], BF16)
    IKT = dscr("IKT", [32, S], BF16)
    YAT = dscr("YAT", [512, S], BF16)
    YBT = dscr("YBT", [512, S], BF16)
    DBG = dscr("DBG", [128, 4096], F32) if debug else None

    Sx = Sched(nc, es)
    al = Alloc(nc)
    PS = [es.enter_context(nc.psum_tensor("ps%d" % i, [128, 512], F32)) for i in range(8)]

    def psb(i):
        return ("ps", i)

    def ps_bf(i):
        return PS[i][:].bitcast(BF16)

    ident = al("ident", [128, 128], BF16)
    identf = al("identf", [128, 128], F32)
    onesf = al("onesf", [128, 128], F32)
    Umat = al("Umat", [128, 128], F32)
    modT = al("modT", [128, 72], F32)
    coefA = al("coefA", [128, 3, KC], F32)
    coefG = al("coefG", [128, 3, KC], F32)
    cosA = al("cosA", [128, NT, 8], F32)
    sinA = al("sinA", [128, NT, 8], F32)
    cosI = al("cosI", [128, NT, 4], F32)
    sinI = al("sinI", [128, NT, 4], F32)
    Fk = al("Fk", [128, NT, 8], F32)
    Fprev = al("Fprev", [128, NT + 1, 8], F32)
    LO = al("LO", [128, NT, 8], F32)
    HI = al("HI", [128, NT, 8], F32)
    fb_bc = al("fb_bc", [128, 8], F32)
    fqg_bc = al("fqg_bc", [128, 128], F32)
    dqg_bc = al("dqg_bc", [128, 128], F32)
    pers_mark = al.mark()

    Sx.enabled = "0" in phases
    cast_order = ["f1w1", "f1w3", "f1w2", "wg", "wt", "wbf", "wbd", "wo", "f2w1", "f2w3", "f2w2"]
    for k in cast_order:
        n = wf[k].shape[1]
        step = 8192
        for c0 in range(0, n, step):
            c1 = min(n, c0 + step)
            Sx.dma("pool", wb[k][:, c0:c1], wf[k][:, c0:c1], (), [("W", k)] if c1 == n else [("Wpart", k, c0)])

    def w_ready(k):
        n = wf[k].shape[1]
        return [("W", k)] + [("Wpart", k, c0) for c0 in range(0, n, 8192) if min(n, c0 + 8192) != n]

    Sx.memset("pool", identf[:], 1.0, ["identf"])
    Sx.op("pool", lambda h: h.affine_select(out=identf[:], in_=identf[:], pattern=[[-1, 128]], compare_op=ALU.is_equal,
                                            fill=0.0, base=0, channel_multiplier=1), ["identf"], ["identf"])
    Sx.cp("dve", ident[:], identf[:], ["identf"], ["ident"])
    Sx.memset("dve", onesf[:], 1.0, ["onesf"])
    Sx.memset("pool", Umat[:], 1.0, ["Umat"])
    Sx.op("pool", lambda h: h.affine_select(out=Umat[:], in_=Umat[:], pattern=[[1, 128]], compare_op=ALU.is_ge,
                                            fill=0.0, base=0, channel_multiplier=-1), ["Umat"], ["Umat"])
    Sx.memset("dve", Fprev[:, 0, :], 0.0, ["Fprev"])
    Sx.dma("sp", fb_bc[:], fbias.partition_broadcast(128), (), ["fb_bc"])
    Sx.dma("sp", fqg_bc[:], fqg.partition_broadcast(128), (), ["fqg_bc"])
    Sx.dma("sp", dqg_bc[:], dqg.partition_broadcast(128), (), ["dqg_bc"])

    m0 = al.mark()
    cond = al("cond", [128, KC], F32)
    ngT = al("ngT", [128, 3, KC], F32)
    abT = al("abT", [128, 72], F32)
    posi = al("posi", [128, NT], I32)
    posf = al("posf", [128, NT], F32)
    adaw = [al("adaw%d" % i, [128, KC, 512], F32) for i in range(2)]
    Sx.dma("sp", cond[:], cT, (), ["cond"])
    Sx.dma("sp", ngT[:], norm_gT, (), ["ngT"])
    Sx.dma("sp", abT[:], ada_bT, (), ["abT"])
    Sx.dma("sp", posi[:], pos, (), ["posi"])
    Sx.act(cond[:], cond[:], AF.Silu, ["cond"], ["cond"])
    for g in range(18):
        sl = adaw[g % 2]
        Sx.dma("sp", sl[:], ada_w[:, :, g * 512:(g + 1) * 512], (), [("adaw", g % 2)])
        for cc in range(4):
            c = g * 4 + cc
            for kc in range(KC):
                Sx.mm(PS[0][:, c:c + 1], sl[:, kc, cc * 128:(cc + 1) * 128], cond[:, kc:kc + 1], kc == 0, kc == KC - 1,
                      [("adaw", g % 2), "cond"], [psb(0)])
    Sx.tt("dve", modT[:], PS[0][:, 0:72], abT[:], ALU.add, [psb(0), "abT"], ["modT"])
    for i in range(3):
        Sx.stt("dve", coefA[:, i, :], modT[:, (i * 3 + 1) * 8:(i * 3 + 2) * 8], 1.0, ngT[:, i, :], ALU.add, ALU.mult,
               ["modT", "ngT"], ["coef"])
        Sx.ts("dve", coefG[:, i, :], modT[:, (i * 3 + 2) * 8:(i * 3 + 3) * 8], 0.5 if i != 1 else 1.0, 0.0, ALU.mult, ALU.add,
              ["modT"], ["coef"])

    def shiftcol(i, kc):
        return modT[:, i * 24 + kc:i * 24 + kc + 1]

    Sx.cp("dve", posf[:], posi[:], ["posi"], ["posf"])
    TWO_PI = 2.0 * math.pi
    C1 = 6.28125
    _c2 = np.array([TWO_PI - C1], dtype=np.float32)
    _c2 = (_c2.view(np.uint32) & np.uint32(0xFFFFE000)).view(np.float32)
    C2 = float(_c2[0])
    C3 = float(TWO_PI - C1 - C2)
    PI_LO = 3.1415925
    for (nf, rot, ctab, stab) in ((8, 16, cosA, sinA), (4, 8, cosI, sinI)):
        mr = al.mark()
        ang = al("ang", [128, NT, nf], F32)
        kf = al("kf", [128, NT * nf], F32)
        ki = al("ki", [128, NT * nf], I32)
        rr = al("rr", [128, NT * nf], F32)
        t1 = al("t1", [128, NT * nf], F32)
        inv = (np.float32(500000.0) ** (-np.arange(0, rot, 2, dtype=np.float32) / np.float32(rot))).astype(np.float32)
        for j in range(nf):
            Sx.ts("dve", ang[:, :, j], posf[:], float(inv[j]), 0.0, ALU.mult, ALU.add, ["posf"], ["ang"])
        a2 = ang[:].rearrange("p a b -> p (a b)")
        Sx.ts("dve", kf[:], a2, 1.0 / TWO_PI, 0.0, ALU.mult, ALU.add, ["ang"], ["kf"])
        Sx.cp("dve", ki[:], kf[:], ["kf"], ["ki"])
        Sx.cp("dve", kf[:], ki[:], ["ki"], ["kf"])
        Sx.stt("dve", rr[:], kf[:], -C1, a2, ALU.mult, ALU.add, ["kf", "ang"], ["rr"])
        Sx.stt("dve", rr[:], kf[:], -C2, rr[:], ALU.mult, ALU.add, ["kf", "rr"], ["rr"])
        Sx.stt("dve", rr[:], kf[:], -C3, rr[:], ALU.mult, ALU.add, ["kf", "rr"], ["rr"])
        Sx.ts("dve", t1[:], rr[:], math.pi, -TWO_PI, ALU.is_gt, ALU.mult, ["rr"], ["t1"])
        Sx.tt("dve", rr[:], rr[:], t1[:], ALU.add, ["rr", "t1"], ["rr"])
        Sx.ts("dve", t1[:], rr[:], -math.pi, TWO_PI, ALU.is_lt, ALU.mult, ["rr"], ["t1"])
        Sx.tt("dve", rr[:], rr[:], t1[:], ALU.add, ["rr", "t1"], ["rr"])
        Sx.ts("dve", kf[:], rr[:], -PI_LO, PI_LO, ALU.max, ALU.min, ["rr"], ["kf"])
        Sx.act(stab[:].rearrange("p a b -> p (a b)"), kf[:], AF.Sin, ["kf"], ["rope"])
        Sx.ts("dve", rr[:], rr[:], math.pi / 2, 0.0, ALU.add, ALU.add, ["rr"], ["rr"])
        Sx.ts("dve", t1[:], rr[:], math.pi, -TWO_PI, ALU.is_gt, ALU.mult, ["rr"], ["t1"])
        Sx.tt("dve", rr[:], rr[:], t1[:], ALU.add, ["rr", "t1"], ["rr"])
        Sx.ts("dve", t1[:], rr[:], -PI_LO, PI_LO, ALU.max, ALU.min, ["rr"], ["t1"])
        Sx.act(ctab[:].rearrange("p a b -> p (a b)"), t1[:], AF.Sin, ["t1"], ["rope"])
        Sx.barrier()
        al.release(mr)
    Sx.barrier()
    al.release(m0)

    class Ring:
        def __init__(self, name, nslots, nel):
            self.name = name
            self.tiles = [al("%s%d" % (name, i), [128, nel], BF16) for i in range(nslots)]
            self.k = 0

        def load(self, key, c0, shape):
            i = self.k % len(self.tiles)
            self.k += 1
            nel = int(np.prod(shape))
            v2 = self.tiles[i][:, 0:nel]
            Sx.dma("sp", v2, wb[key][:, c0:c0 + nel], w_ready(key), [(self.name, i)])
            return _view(v2, shape), (self.name, i)

    def norm_mod(i, xT_sb, hT_sb, sq, tmp, rs, rstd):
        for kc in range(KC):
            Sx.act(sq[kc % 2][:], xT_sb[:, kc, :], AF.Square, [("xT", kc)], [("sq", kc % 2)])
            Sx.mm(PS[7][:], onesf[:], sq[kc % 2][:], kc == 0, kc == KC - 1, [("sq", kc % 2), "onesf"], [psb(7)])
        Sx.act(rs[:], PS[7][:], AF.Sqrt, [psb(7)], ["rs"], bias=EPS, scale=1.0 / D)
        Sx.recip(rstd[:], rs[:], ["rs"], ["rstd"])
        for kc in range(KC):
            Sx.stt("dve", tmp[kc % 2][:], xT_sb[:, kc, :], coefA[:, i, kc:kc + 1], rstd[:], ALU.mult, ALU.mult,
                   [("xT", kc), "rstd", "coef"], [("tmp", kc % 2)])
            Sx.act(hT_sb[:, kc, :], tmp[kc % 2][:], AF.Identity, [("tmp", kc % 2), "modT"], [("hT", kc)],
                   bias=shiftcol(i, kc), scale=1.0)

    def ffn(i, pfx, ring, xT_sb, hT_sb, aT, sil):
        hreads = [("hT", kc) for kc in range(KC)]
        for j0 in range(0, NJ, 4):
            nj = min(4, NJ - j0)
            w1p, r1 = ring.load(pfx + "w1", j0 * KC * 128, [nj, KC, 128])
            w3p, r3 = ring.load(pfx + "w3", j0 * KC * 128, [nj, KC, 128])
            for jj in range(nj):
                j = j0 + jj
                ub, vb = (j % 2), 2 + (j % 2)
                for kc in range(KC):
                    Sx.mm(PS[ub][:], w1p[:, jj, kc, :], hT_sb[:, kc, :], kc == 0, kc == KC - 1, [r1, ("hT", kc)], [psb(ub)])
                for kc in range(KC):
                    Sx.mm(PS[vb][:], w3p[:, jj, kc, :], hT_sb[:, kc, :], kc == 0, kc == KC - 1, [r3, ("hT", kc)], [psb(vb)])
                Sx.act(sil[j % 2][:], PS[ub][:], AF.Silu, [psb(ub)], [("sil", j % 2)])
                Sx.tt("dve", aT[:, j, :], sil[j % 2][:], PS[vb][:], ALU.mult, [("sil", j % 2), psb(vb)], [("aT", j)])
        for m in range(KC):
            w2p, r2 = ring.load(pfx + "w2", m * NJ * 128, [NJ, 128])
            yb = 4 + (m % 2)
            for j in range(NJ):
                Sx.mm(PS[yb][:], w2p[:, j, :], aT[:, j, :], j == 0, j == NJ - 1, [r2, ("aT", j)], [psb(yb)])
            Sx.stt("dve", xT_sb[:, m, :], PS[yb][:], coefG[:, i, m:m + 1], xT_sb[:, m, :], ALU.mult, ALU.add,
                   [psb(yb), ("xT", m), "coef"], [("xT", m)])

    Sx.enabled = "A" in phases
    mA = al.mark()
    ring = Ring("wr", 6, 4096)
    xT_sb = al("xT_sb", [128, KC, 512], F32)
    hT_sb = al("hT_sb", [128, KC, 512], BF16)
    sq = [al("sq%d" % i, [128, 512], F32) for i in range(2)]
    tmp = [al("tmp%d" % i, [128, 512], F32) for i in range(2)]
    rs = al("rs", [128, 512], F32)
    rstd = al("rstd", [128, 512], F32)
    aT = al("aT", [128, NJ, 512], BF16)
    sil = [al("sil%d" % i, [128, 512], F32) for i in range(2)]
    sg = al("sg", [128, 16, 512], BF16)
    zq = al("zq", [128, 512], F32)
    zn = al("zn", [128, 512], F32)
    zr = al("zr", [128, 512], F32)
    qb = al("qb", [128, 512], BF16)
    ssq = al("ssq", [128, 8], F32)
    rq = al("rq", [128, 8], F32)
    rt = [al("rt%d" % i, [128, 64], F32) for i in range(4)]
    qTst = al("qTst", [128, 4, 512], BF16)
    vst = al("vst", [128, 4, 512], BF16)
    zm = al("zm", [128, 432], F32)
    lf = al("lf", [128, 8], F32)
    lf2 = al("lf2", [128, 8], F32)
    wsc = al("wsc", [128, 8], F32)
    dkb = al("dkb", [128, 64], BF16)
    iqb = al("iqb", [128, 256], BF16)
    iqf = al("iqf", [128, 256], F32)
    ikb = al("ikb", [128, 32], BF16)
    ikf = al("ikf", [128, 32], F32)
    dkst = al("dkst", [64, 512], BF16)
    iqst = al("iqst", [128, 2, 512], BF16)
    ikst = al("ikst", [32, 512], BF16)
    dvst = al("dvst", [128, 4, 64], BF16)

    X1Tv = X1T.rearrange("(kc p) t -> p kc t", p=128)
    xTv = xT.rearrange("(kc p) t -> p kc t", p=128)
    outTv = outT.rearrange("(kc p) t -> p kc t", p=128)
    SGTv = SGT.rearrange("(c p) t -> p c t", p=128)
    xres = [("xT", kc) for kc in range(KC)]
    hres = [("hT", kc) for kc in range(KC)]
    WSCALE = (8 ** -0.5) * (32 ** -0.5)

    def qknorm(src_ps_ap, nh, gain_ap, dst, gdst_res, resr):
        n = nh * 64
        Sx.act(zq[:, 0:n], src_ps_ap, AF.Square, resr, ["zq"])
        Sx.op("dve", lambda h: h.tensor_reduce(ssq[:, 0:nh], zq[:, 0:n].rearrange("p (h d) -> p h d", h=nh), AX.X, ALU.add),
              ["zq"], ["ssq"])
        Sx.act(rq[:, 0:nh], ssq[:, 0:nh], AF.Sqrt, ["ssq"], ["rq"], bias=EPS, scale=1.0 / 64)
        Sx.recip(rq[:, 0:nh], rq[:, 0:nh], ["rq"], ["rq"])
        Sx.tt("dve", zn[:, 0:n].rearrange("p (h d) -> p h d", h=nh), src_ps_ap.rearrange("p (h d) -> p h d", h=nh),
              rq[:, 0:nh].unsqueeze(2).to_broadcast([128, nh, 64]), ALU.mult, resr + ["rq"], ["zn"])
        Sx.tt("dve", dst.rearrange("p (h d) -> p h d", h=nh), zn[:, 0:n].rearrange("p (h d) -> p h d", h=nh),
              gain_ap.unsqueeze(1).to_broadcast([128, nh, 64]), ALU.mult, ["zn", "gains"], gdst_res)

    def rope(src, dst, nh, dh, half, ctab, stab, tile, resr, resw):
        s3 = src.rearrange("p (h d) -> p h d", h=nh)
        d3 = dst.rearrange("p (h d) -> p h d", h=nh)
        c = ctab[:, tile, :].unsqueeze(1).to_broadcast([128, nh, half])
        s = stab[:, tile, :].unsqueeze(1).to_broadcast([128, nh, half])
        x1 = s3[:, :, 0:half]
        x2 = s3[:, :, half:2 * half]
        tv = [rt[k][:, 0:nh * half].rearrange("p (h d) -> p h d", h=nh) for k in range(4)]
        Sx.tt("dve", tv[0], x1, c, ALU.mult, resr + ["rope"], ["rt0"])
        Sx.tt("dve", tv[1], x2, s, ALU.mult, resr + ["rope"], ["rt1"])
        Sx.tt("dve", tv[2], x2, c, ALU.mult, resr + ["rope"], ["rt2"])
        Sx.tt("dve", tv[3], x1, s, ALU.mult, resr + ["rope"], ["rt3"])
        Sx.tt("dve", d3[:, :, 0:half], tv[0], tv[1], ALU.subtract, ["rt0", "rt1"], resw)
        Sx.tt("dve", d3[:, :, half:2 * half], tv[2], tv[3], ALU.add, ["rt2", "rt3"], resw)
        Sx.cp("dve", d3[:, :, 2 * half:dh], s3[:, :, 2 * half:dh], resr, resw)

    for tb in range(NB):
        t0 = tb * 512
        Sx.dma("sp", xT_sb[:], xTv[:, :, t0:t0 + 512], (), xres)
        norm_mod(0, xT_sb, hT_sb, sq, tmp, rs, rstd)
        ffn(0, "f1", ring, xT_sb, hT_sb, aT, sil)
        Sx.dma("pool", X1Tv[:, :, t0:t0 + 512], xT_sb[:], xres, [("X1T", tb)])
        norm_mod(1, xT_sb, hT_sb, sq, tmp, rs, rstd)
        for cg in range(4):
            wgp, rg = ring.load("wg", cg * 4 * KC * 128, [4, KC, 128])
            for cc in range(4):
                c = cg * 4 + cc
                gbk = c % 2
                for kc in range(KC):
                    Sx.mm(PS[gbk][:], wgp[:, cc, kc, :], hT_sb[:, kc, :], kc == 0, kc == KC - 1, [rg, ("hT", kc)], [psb(gbk)])
                Sx.act(sg[:, c, :], PS[gbk][:], AF.Sigmoid, [psb(gbk)], [("sg", c)])
        Sx.dma("pool", SGTv[:, :, t0:t0 + 512], sg[:], [("sg", c) for c in range(16)], [("SGT", tb)])
        for g in range(5):
            wtp, rw = ring.load("wt", g * KC * 512, [KC, 512])
            ncol = 512 if g < 4 else 432
            for s in range(4):
                tile = tb * 4 + s
                zb = (g * 4 + s) % 4
                tb_ = 4 + (g * 4 + s) % 3
                for kc in range(KC):
                    Sx.mm(PS[zb][:, 0:ncol], hT_sb[:, kc, s * 128:(s + 1) * 128], wtp[:, kc, 0:ncol], kc == 0, kc == KC - 1,
                          [rw, ("hT", kc)], [psb(zb)])
                z = PS[zb]
                if g in (0, 1, 3):
                    gain = (fqg_bc[:, 0:64], fqg_bc[:, 64:128], None, dqg_bc[:, 0:64])[g]
                    if g == 3:
                        qknorm(z[:, 0:512], 8, gain, zr[:, 0:512], ["zr"], [psb(zb)])
                        rope(zr[:, 0:512], qb[:, 0:512], 8, 64, 8, cosA, sinA, tile, ["zr"], ["qb"])
                    else:
                        qknorm(z[:, 0:512], 8, gain, qb[:, 0:512], ["qb"], [psb(zb)])
                    for c in range(4):
                        Sx.tr(ps_bf(tb_)[:, c * 128:(c + 1) * 128], qb[:, c * 128:(c + 1) * 128], ident[:], ["qb", "ident"], [psb(tb_)])
                    Sx.cp("act", qTst[:, :, s * 128:(s + 1) * 128], ps_bf(tb_)[:, 0:512].rearrange("p (c t) -> p c t", c=4),
                          [psb(tb_)], [("qTst", s)])
                elif g == 2:
                    Sx.cp("act", vst[:, s, :], z[:, 0:512], [psb(zb)], [("vst", s)])
                else:
                    Sx.cp("act", zm[:], z[:, 0:432], [psb(zb)], ["zm"])
                    Sx.tt("dve", lf[:], zm[:, 0:8], fb_bc[:], ALU.add, ["zm", "fb_bc"], ["lf"])
                    Sx.act(lf[:], lf[:], AF.Exp, ["lf"], ["lf"], scale=-1.0)
                    Sx.act(lf[:], lf[:], AF.Ln, ["lf"], ["lf"], bias=1.0)
                    Sx.ts("dve", lf2[:], lf[:], -1.0, 0.0, ALU.mult, ALU.add, ["lf"], ["lf2"])
                    Sx.mm(PS[7][:, 0:8], Umat[:], lf2[:], True, True, ["lf2", "Umat"], [psb(7)])
                    Sx.mm(PS[7][:, 8:16], onesf[:], lf2[:], True, True, ["lf2", "onesf"], [psb(7)])
                    Sx.tt("dve", Fk[:, tile, :], PS[7][:, 0:8], Fprev[:, tile, :], ALU.add, [psb(7), "Fprev"], ["Fk"])
                    Sx.tt("dve", Fprev[:, tile + 1, :], PS[7][:, 8:16], Fprev[:, tile, :], ALU.add, [psb(7), "Fprev"], ["Fprev"])
                    qknorm(zm[:, 8:72], 1, dqg_bc[:, 64:128], zr[:, 0:64], ["zr"], ["zm"])
                    rope(zr[:, 0:64], dkb[:], 1, 64, 8, cosA, sinA, tile, ["zr"], ["dkb"])
                    Sx.cp("dve", dvst[:, s, :], zm[:, 72:136], ["zm"], [("dvst", s)])
                    Sx.ts("dve", wsc[:], zm[:, 424:432], WSCALE, 0.0, ALU.mult, ALU.add, ["zm"], ["wsc"])
                    Sx.stt("dve", LO[:, tile, :], wsc[:], -1.0, wsc[:], ALU.mult, ALU.max, ["wsc"], ["LOHI"])
                    Sx.ts("dve", HI[:, tile, :], wsc[:], 0.0, 2.0, ALU.is_ge, ALU.mult, ["wsc"], ["LOHI"])
                    Sx.ts("dve", HI[:, tile, :], HI[:, tile, :], -1.0, 0.0, ALU.add, ALU.add, ["LOHI"], ["LOHI"])
                    rope(zm[:, 136:392], iqf[:], 8, 32, 4, cosI, sinI, tile, ["zm"], ["iqf"])
                    Sx.cp("dve", iqb[:], iqf[:], ["iqf"], ["iqb"])
                    rope(zm[:, 392:424], ikf[:], 1, 32, 4, cosI, sinI, tile, ["zm"], ["ikf"])
                    Sx.cp("dve", ikb[:], ikf[:], ["ikf"], ["ikb"])
                    pb = ps_bf(tb_)
                    Sx.tr(pb[0:64, 0:128], dkb[:], ident[:], ["dkb", "ident"], [psb(tb_)])
                    Sx.tr(pb[:, 128:256], iqb[:, 0:128], ident[:], ["iqb", "ident"], [psb(tb_)])
                    Sx.tr(pb[:, 256:384], iqb[:, 128:256], ident[:], ["iqb", "ident"], [psb(tb_)])
                    Sx.tr(pb[0:32, 384:512], ikb[:], ident[:], ["ikb", "ident"], [psb(tb_)])
                    Sx.cp("act", dkst[:, s * 128:(s + 1) * 128], pb[0:64, 0:128], [psb(tb_)], [("dkst", s)])
                    Sx.cp("act", iqst[:, :, s * 128:(s + 1) * 128], pb[:, 128:384].rearrange("p (c t) -> p c t", c=2),
                          [psb(tb_)], [("iqst", s)])
                    Sx.cp("act", ikst[:, s * 128:(s + 1) * 128], pb[0:32, 384:512], [psb(tb_)], [("ikst", s)])
            if g in (0, 1, 3):
                dst = (FQT, FKT, None, DQT)[g]
                Sx.dma("pool", dst.rearrange("(c p) t -> p c t", p=128)[:, :, t0:t0 + 512], qTst[:],
                       [("qTst", s) for s in range(4)], [("QKT", g, tb)])
            elif g == 2:
                Sx.dma("pool", FV.rearrange("(s p) c -> p s c", p=128)[:, tb * 4:(tb + 1) * 4, :], vst[:],
                       [("vst", s) for s in range(4)], [("FV", tb)])
            else:
                Sx.dma("pool", DKT[:, t0:t0 + 512], dkst[:], [("dkst", s) for s in range(4)], [("DKT", tb)])
                Sx.dma("pool", IQT.rearrange("(c p) t -> p c t", p=128)[:, :, t0:t0 + 512], iqst[:],
                       [("iqst", s) for s in range(4)], [("IQT", tb)])
                Sx.dma("pool", IKT[:, t0:t0 + 512], ikst[:], [("ikst", s) for s in range(4)], [("IKT", tb)])
                Sx.dma("pool", DV.rearrange("(s p) c -> p s c", p=128)[:, tb * 4:(tb + 1) * 4, :], dvst[:],
                       [("dvst", s) for s in range(4)], [("DV", tb)])
    Sx.barrier()
    al.release(mA)

    Sx.enabled = "B1" in phases
    mB = al.mark()
    KT = al("KT", [128, S], BF16)
    QT = al("QT", [128, S], BF16)
    Vaug = al("Vaug", [128, NT, 2, 128], BF16)
    QB = 256
    NQ = S // QB
    NDG = QB // 128
    CM = [al("CM%d" % j, [128, QB], BF16) for j in range(NDG)]
    bq = [al("bq%d" % i, [128, NT], F32) for i in range(2)]
    pT = [[al("pT%d_%d" % (h2, i), [128, QB], BF16) for i in range(3)] for h2 in range(2)]
    rc = al("rc", [128, QB], F32)
    yo = [[al("yo%d_%d" % (h2, i), [128, QB], BF16) for i in range(2)] for h2 in range(2)]
    Sx.memset("pool", Vaug[:, :, :, 64:128], 1.0, ["Vaug1"])
    for j in range(NDG):
        Sx.memset("pool", CM[j][:], 1.0, [("CM", j)])
        Sx.op("pool", (lambda jj: (lambda h: h.affine_select(out=CM[jj][:], in_=CM[jj][:], pattern=[[1, QB]],
                                                              compare_op=ALU.is_ge, fill=0.0, base=-128 * jj,
                                                              channel_multiplier=-1)))(j), [("CM", j)], [("CM", j)])
    FVv = FV.rearrange("(kb p) c -> p kb c", p=128)
    for hp in range(4):
        Sx.dma("sp", KT[:], FKT[hp * 128:(hp + 1) * 128, :], (), ["KT"])
        Sx.dma("sp", QT[:], FQT[hp * 128:(hp + 1) * 128, :], (), ["QT"])
        for h2 in range(2):
            Sx.dma("sp", Vaug[:, :, h2, 0:64], FVv[:, :, hp * 128 + h2 * 64:hp * 128 + h2 * 64 + 64], (), [("Vaug", h2)])
        for Q in range(NQ):
            nkb = NDG * (Q + 1)
            obs = [4 + 2 * (Q % 2), 5 + 2 * (Q % 2)]
            for h2 in range(2):
                h = hp * 2 + h2
                Sx.ts("dve", bq[h2][:, 0:nkb], Fk[:, 0:nkb, h], Fprev[:, NDG * Q + NDG // 2, h:h + 1], -1.0, ALU.subtract, ALU.mult,
                      ["Fk", "Fprev"], [("bq", h2)])

            def qk(kb):
                for h2 in range(2):
                    b0 = h2 * 64
                    sbk = (kb % 2) * 2 + h2
                    Sx.mm(PS[sbk][:, 0:QB], KT[b0:b0 + 64, kb * 128:(kb + 1) * 128],
                          QT[b0:b0 + 64, Q * QB:(Q + 1) * QB], True, True, ["KT", "QT"], [psb(sbk)])

            def rest(kb):
                for h2 in range(2):
                    p_ = pT[h2][kb % 3]
                    sbk = (kb % 2) * 2 + h2
                    Sx.act(p_[:], PS[sbk][:, 0:QB], AF.Exp, [psb(sbk), ("bq", h2)], [("pT", h2, kb % 3)],
                           bias=bq[h2][:, kb:kb + 1], scale=0.125)
                    if kb >= NDG * Q:
                        Sx.tt("pool", p_[:], p_[:], CM[kb - NDG * Q][:], ALU.mult,
                              [("pT", h2, kb % 3), ("CM", kb - NDG * Q)], [("pT", h2, kb % 3)])
                for h2 in range(2):
                    Sx.mm(PS[obs[h2]][:, 0:QB], Vaug[:, kb, h2, :], pT[h2][kb % 3][:], kb == 0, kb == nkb - 1,
                          [("pT", h2, kb % 3), ("Vaug", h2), "Vaug1"], [psb(obs[h2])])

            qk(0)
            for kb in range(nkb):
                if kb + 1 < nkb:
                    qk(kb + 1)
                rest(kb)
            for h2 in range(2):
                h = hp * 2 + h2
                ob = obs[h2]
                Sx.recip(rc[64:128, :], PS[ob][64:128, 0:QB], [psb(ob)], ["rc"])
                Sx.tt("dve", yo[h2][Q % 2][0:64, :], PS[ob][0:64, 0:QB], rc[64:128, :], ALU.mult, [psb(ob), "rc"], [("yo", h2, Q % 2)])
                Sx.dma("pool", YAT[h * 64:(h + 1) * 64, Q * QB:(Q + 1) * QB], yo[h2][Q % 2][0:64, :], [("yo", h2, Q % 2)], [("YAT", h, Q)])
    Sx.barrier()
    al.release(mB)

    Sx.enabled = "B2" in phases
    mD = al.mark()
    dkT = al("dkT", [64, S], BF16)
    ikT = al("ikT", [32, S], BF16)
    dvaug = al("dvaug", [128, NT, 128], BF16)
    score = al("score", [128, S], F32)
    maskb = [al("maskb%d" % i, [128, S], BF16) for i in range(2)]
    maskT = al("maskT", [128, NT, 128], BF16)
    iqTi = [al("iqTi%d" % i, [32, 8, 128], BF16) for i in range(2)]
    dqTi = [al("dqTi%d" % i, [64, 1024], BF16) for i in range(2)]
    tmpc = [al("tmpc%d" % i, [128, 512], F32) for i in range(4)]
    sc2 = [al("sc2_%d" % i, [128, 512], F32) for i in range(2)]
    tmp2 = [al("tmp2_%d" % i, [128, 512], F32) for i in range(2)]
    NEGD = al("NEGD", [128, 128], F32)
    pow2 = al("pow2", [128, NIT], F32)
    am = al("am", [128, 1], F32)
    step = al("step", [128, NIT], F32)
    lo = al("lo", [128, NIT + 1], F32)
    tpt = al("tpt", [128, NIT + 1], F32)
    cnt = al("cnt", [128, NIT], F32)
    ge = al("ge", [128, NIT], F32)
    pTd = [al("pTd%d" % i, [128, 1024], BF16) for i in range(2)]
    rc2 = al("rc2", [128, 512], F32)
    ybo = [al("ybo%d" % i, [128, 1024], BF16) for i in range(2)]

    Sx.dma("sp", dkT[:], DKT, (), ["dkT"])
    Sx.dma("sp", ikT[:], IKT, (), ["ikT"])
    Sx.memset("pool", dvaug[:, :, 64:128], 1.0, ["dvaug1"])
    Sx.dma("sp", dvaug[:, :, 0:64], DV.rearrange("(kb p) c -> p kb c", p=128), (), ["dvaug"])
    Sx.memset("dve", NEGD[:], 0.0, ["NEGD"])
    Sx.memset("dve", NEGD[0:64, 64:128], -BIG, ["NEGD"])
    for it in range(NIT):
        Sx.memset("dve", pow2[:, it:it + 1], 2.0 ** (-it), ["pow2"])
    IQTv = IQT.rearrange("(h d) t -> d h t", d=32)
    DQTv = DQT.rearrange("(h d) t -> d h t", d=64)
    YBTv = YBT.rearrange("(h d) t -> d h t", d=64)

    def stage1(i):
        nk = 128 * (i + 1)
        iq_ = iqTi[i % 2]
        Sx.dma("sp", iq_[:], IQTv[:, :, i * 128:(i + 1) * 128], (), [("iqTi", i % 2)])
        n = 0
        c = 0
        for k0 in range(0, nk, 512):
            kn = min(512, nk - k0)
            s2 = sc2[c % 2]
            for h in range(8):
                db = n % 2
                tr_ = tmpc[n % 4]
                tres = ("tmpc", n % 4)
                n += 1
                Sx.mm(PS[db][:, 0:kn], iq_[:, h, :], ikT[:, k0:k0 + kn], True, True, [("iqTi", i % 2), "ikT"], [psb(db)])
                Sx.act(tr_[:, 0:kn], PS[db][:, 0:kn], AF.Relu, [psb(db), "LOHI"], [tres], scale=LO[:, i, h:h + 1])
                if h == 0:
                    Sx.ts("dve", score[:, k0:k0 + kn], tr_[:, 0:kn], HI[:, i, 0:1], 0.0, ALU.mult, ALU.add,
                          [tres, "LOHI"], [("score", k0)])
                elif h < NH_DVE:
                    Sx.stt("dve", score[:, k0:k0 + kn], tr_[:, 0:kn], HI[:, i, h:h + 1], score[:, k0:k0 + kn], ALU.mult, ALU.add,
                           [tres, ("score", k0), "LOHI"], [("score", k0)])
                elif h == NH_DVE:
                    Sx.act(s2[:, 0:kn], tr_[:, 0:kn], AF.Copy, [tres, "LOHI"], [("sc2", c % 2)], scale=HI[:, i, h:h + 1])
                else:
                    t2 = tmp2[h % 2]
                    Sx.act(t2[:, 0:kn], tr_[:, 0:kn], AF.Copy, [tres, "LOHI"], [("tmp2", h % 2)], scale=HI[:, i, h:h + 1])
                    Sx.tt("pool", s2[:, 0:kn], s2[:, 0:kn], t2[:, 0:kn], ALU.add, [("tmp2", h % 2), ("sc2", c % 2)], [("sc2", c % 2)])
            if NH_DVE < 8:
                Sx.tt("dve", score[:, k0:k0 + kn], score[:, k0:k0 + kn], s2[:, 0:kn], ALU.add,
                      [("score", k0), ("sc2", c % 2)], [("score", k0)])
            c += 1

    def stage2(i):
        nk = 128 * (i + 1)
        sres = [("score", k0) for k0 in range(0, nk, 512)]
        mb = maskb[i % 2]
        Sx.op("dve", lambda h: h.tensor_reduce(am[:], score[:, 0:nk], AX.X, ALU.max, apply_absolute_value=True), sres, ["am"])
        Sx.tt("dve", score[:, nk - 128:nk], score[:, nk - 128:nk], NEGD[:], ALU.add, sres + ["NEGD"], sres)
        Sx.ts("dve", am[:], am[:], 1.0, 0.0, ALU.add, ALU.add, ["am"], ["am"])
        Sx.ts("dve", step[:], pow2[:], am[:, 0:1], 0.0, ALU.mult, ALU.add, ["am", "pow2"], ["step"])
        Sx.memset("dve", tpt[:, 0:1], 0.0, ["tpt"])
        Sx.memset("dve", cnt[:], 0.0, ["cnt"])
        for it in range(NIT):
            Sx.ts("dve", mb[:, 0:nk], score[:, 0:nk], tpt[:, it:it + 1], 0.0, ALU.is_ge, ALU.add, sres + ["tpt"],
                  [("maskb", i % 2)], accum=cnt[:, it:it + 1])
            Sx.ts("dve", ge[:, it:it + 1], cnt[:, it:it + 1], float(TOPK) - 0.5, -0.5, ALU.is_ge, ALU.add, [("maskb", i % 2), "cnt"], ["ge"])
            Sx.stt("dve", tpt[:, it + 1:it + 2], ge[:, it:it + 1], step[:, it:it + 1], tpt[:, it:it + 1], ALU.mult, ALU.add,
                   ["ge", "step", "tpt"], ["tpt"])
        Sx.stt("dve", lo[:, 0:1], step[:, NIT - 1:NIT], -0.5, tpt[:, NIT:NIT + 1], ALU.mult, ALU.add, ["step", "tpt"], ["lo"])
        Sx.ts("dve", mb[:, 0:nk], score[:, 0:nk], lo[:, 0:1], 0.0, ALU.is_ge, ALU.add, sres + ["lo"], [("maskb", i % 2)])

    def stage3(i):
        nkb = i + 1
        mb = maskb[i % 2]
        dq_ = dqTi[i % 2]
        Sx.dma("sp", dq_[:].rearrange("p (h t) -> p h t", h=8), DQTv[:, :, i * 128:(i + 1) * 128], (), [("dqTi", i % 2)])
        for gi, g0 in enumerate(range(0, nkb, 4)):
            gn = min(4, nkb - g0)
            tbk = gi % 2
            pb = ps_bf(tbk)
            for kk in range(gn):
                kb = g0 + kk
                Sx.tr(pb[:, kk * 128:(kk + 1) * 128], mb[:, kb * 128:(kb + 1) * 128], ident[:], [("maskb", i % 2), "ident"], [psb(tbk)])
            Sx.cp("act", maskT[:, g0:g0 + gn, :], pb[:, 0:gn * 128].rearrange("p (c t) -> p c t", c=gn), [psb(tbk)], [("maskT", g0 // 4)])

        def qk(kb):
            for half in range(2):
                sb_ = 2 + (2 * kb + half) % 4
                Sx.mm(PS[sb_][:], dkT[:, kb * 128:(kb + 1) * 128], dq_[:, half * 512:(half + 1) * 512], True, True,
                      ["dkT", ("dqTi", i % 2)], [psb(sb_)])

        def rest(kb):
            sl = kb % 2
            for half in range(2):
                sb_ = 2 + (2 * kb + half) % 4
                Sx.act(pTd[sl][:, half * 512:(half + 1) * 512], PS[sb_][:], AF.Exp, [psb(sb_)], [("pTd", sl, half)], scale=0.125)
            Sx.tt("pool", pTd[sl][:].rearrange("p (h t) -> p h t", h=8), pTd[sl][:].rearrange("p (h t) -> p h t", h=8),
                  maskT[:, kb, :].unsqueeze(1).to_broadcast([128, 8, 128]), ALU.mult,
                  [("pTd", sl, 0), ("pTd", sl, 1), ("maskT", kb // 4)], [("pTd", sl, 0), ("pTd", sl, 1)])
            for half in range(2):
                Sx.mm(PS[6 + half][:], dvaug[:, kb, :], pTd[sl][:, half * 512:(half + 1) * 512], kb == 0, kb == nkb - 1,
                      [("pTd", sl, half), "dvaug", "dvaug1"], [psb(6 + half)])

        qk(0)
        for kb in range(nkb):
            if kb + 1 < nkb:
                qk(kb + 1)
            rest(kb)
        for half in range(2):
            Sx.recip(rc2[64:128, :], PS[6 + half][64:128, :], [psb(6 + half)], ["rc2"])
            Sx.tt("dve", ybo[i % 2][0:64, half * 512:(half + 1) * 512], PS[6 + half][0:64, :], rc2[64:128, :], ALU.mult,
                  [psb(6 + half), "rc2"], [("ybo", i % 2)])
        Sx.dma("pool", YBTv[:, :, i * 128:(i + 1) * 128], ybo[i % 2][0:64, :].rearrange("p (h t) -> p h t", h=8),
               [("ybo", i % 2)], [("YBT", i)])

    stage1(0)
    stage2(0)
    for i in range(NT):
        if i + 1 < NT:
            stage1(i + 1)
            stage2(i + 1)
        stage3(i)
    Sx.barrier()
    al.release(mD)

    Sx.enabled = "C" in phases
    ring2 = Ring("wr2", 6, 4096)
    xT_sb = al("xT_c", [128, KC, 512], F32)
    hT_sb = al("hT_c", [128, KC, 512], BF16)
    sq = [al("sqc%d" % i, [128, 512], F32) for i in range(2)]
    tmp = [al("tmpc_%d" % i, [128, 512], F32) for i in range(2)]
    rs = al("rsc", [128, 512], F32)
    rstd = al("rstdc", [128, 512], F32)
    aT = al("aTc", [128, NJ, 512], BF16)
    sil = [al("silc%d" % i, [128, 512], F32) for i in range(2)]
    sgc = al("sgc", [128, 16, 512], BF16)
    yaT = al("yaT", [128, 4, 512], BF16)
    ybT = al("ybT", [128, 4, 512], BF16)
    wbf_sb = al("wbf_sb", [128, 4, D], BF16)
    wbd_sb = al("wbd_sb", [128, 4, D], BF16)
    wo_sb = al("wo_sb", [128, KC, D], BF16)
    mg = al("mg", [128, KC, 512], BF16)
    m1 = [al("m1_%d" % i, [128, 512], F32) for i in range(2)]
    m2 = [al("m2_%d" % i, [128, 512], F32) for i in range(2)]
    Sx.dma("sp", wbf_sb[:].rearrange("p a b -> p (a b)"), wb["wbf"], w_ready("wbf"), ["wbf"])
    Sx.dma("sp", wbd_sb[:].rearrange("p a b -> p (a b)"), wb["wbd"], w_ready("wbd"), ["wbd"])
    Sx.dma("sp", wo_sb[:].rearrange("p a b -> p (a b)"), wb["wo"], w_ready("wo"), ["wo"])
    YATv = YAT.rearrange("(c p) t -> p c t", p=128)
    YBTv2 = YBT.rearrange("(c p) t -> p c t", p=128)
    for tb in range(NB):
        t0 = tb * 512
        Sx.dma("sp", xT_sb[:], X1Tv[:, :, t0:t0 + 512], (), xres)
        Sx.dma("sp", yaT[:], YATv[:, :, t0:t0 + 512], (), ["yaT"])
        Sx.dma("sp", ybT[:], YBTv2[:, :, t0:t0 + 512], (), ["ybT"])
        Sx.dma("sp", sgc[:], SGTv[:, :, t0:t0 + 512], (), ["sgc"])
        for m in range(KC):
            for c in range(4):
                Sx.mm(PS[0][:], wbf_sb[:, c, m * 128:(m + 1) * 128], yaT[:, c, :], c == 0, c == 3, ["wbf", "yaT"], [psb(0)])
            for c in range(4):
                Sx.mm(PS[1][:], wbd_sb[:, c, m * 128:(m + 1) * 128], ybT[:, c, :], c == 0, c == 3, ["wbd", "ybT"], [psb(1)])
            Sx.tt("dve", m1[m % 2][:], PS[0][:], sgc[:, m, :], ALU.mult, [psb(0), "sgc"], [("m1", m % 2)])
            Sx.tt("dve", m2[m % 2][:], PS[1][:], sgc[:, 8 + m, :], ALU.mult, [psb(1), "sgc"], [("m2", m % 2)])
            Sx.tt("pool", mg[:, m, :], m1[m % 2][:], m2[m % 2][:], ALU.add, [("m1", m % 2), ("m2", m % 2)], [("mg", m)])
        for m in range(KC):
            yb_ = 4 + (m % 2)
            for kc in range(KC):
                Sx.mm(PS[yb_][:], wo_sb[:, kc, m * 128:(m + 1) * 128], mg[:, kc, :], kc == 0, kc == KC - 1, ["wo", ("mg", kc)], [psb(yb_)])
            Sx.stt("dve", xT_sb[:, m, :], PS[yb_][:], coefG[:, 1, m:m + 1], xT_sb[:, m, :], ALU.mult, ALU.add,
                   [psb(yb_), ("xT", m), "coef"], [("xT", m)])
        norm_mod(2, xT_sb, hT_sb, sq, tmp, rs, rstd)
        ffn(2, "f2", ring2, xT_sb, hT_sb, aT, sil)
        Sx.dma("pool", outTv[:, :, t0:t0 + 512], xT_sb[:], xres, [("out", tb)])
    Sx.enabled = True
    Sx.barrier()

    with nc.Block() as block:
        Sx.emit(block)
    es.close()
    return nc


def _prep_shared(inp):
    f = np.float32
    d = {}
    ada_w = np.asarray(inp["ada_w"], f)[0]
    d["ada_w"] = np.ascontiguousarray(ada_w.reshape(KC, 128, 9 * D).transpose(1, 0, 2))
    d["ada_bT"] = np.ascontiguousarray(np.asarray(inp["ada_b"], f)[0].reshape(72, 128).T)
    d["norm_gT"] = np.ascontiguousarray(np.asarray(inp["norm_g"], f)[0].reshape(3, KC, 128).transpose(2, 0, 1))
    for p, name in (("f1", "ffn1"), ("f2", "ffn2")):
        w1 = np.asarray(inp[name + "_w1"], f)[0]
        w3 = np.asarray(inp[name + "_w3"], f)[0]
        w2 = np.asarray(inp[name + "_w2"], f)[0]
        d[p + "w1"] = np.ascontiguousarray(w1.reshape(KC, 128, NJ, 128).transpose(1, 2, 0, 3)).reshape(128, -1)
        d[p + "w3"] = np.ascontiguousarray(w3.reshape(KC, 128, NJ, 128).transpose(1, 2, 0, 3)).reshape(128, -1)
        d[p + "w2"] = np.ascontiguousarray(w2.reshape(NJ, 128, KC, 128).transpose(1, 2, 0, 3)).reshape(128, -1)
    w_in = np.asarray(inp["w_in"], f)[0]
    offs = dict(fq=0, fk=512, fv=1024, ff=1536, dq=1544, dk=2056, dv=2120, iq=2184, ik=2440, iw=2472, ga=2480, gb=3504)
    g4 = np.concatenate([w_in[:, offs["ff"]:offs["ff"] + 8], w_in[:, offs["dk"]:offs["dk"] + 64],
                         w_in[:, offs["dv"]:offs["dv"] + 64], w_in[:, offs["iq"]:offs["iq"] + 256],
                         w_in[:, offs["ik"]:offs["ik"] + 32], w_in[:, offs["iw"]:offs["iw"] + 8],
                         np.zeros((D, 80), f)], axis=1)
    groups = [w_in[:, 0:512], w_in[:, 512:1024], w_in[:, 1024:1536], w_in[:, offs["dq"]:offs["dq"] + 512], g4]
    wt = np.stack(groups, 0)
    d["wt"] = np.ascontiguousarray(wt.reshape(5, KC, 128, 512).transpose(2, 0, 1, 3)).reshape(128, -1)
    wg = w_in[:, 2480:4528]
    d["wg"] = np.ascontiguousarray(wg.reshape(KC, 128, 16, 128).transpose(1, 2, 0, 3)).reshape(128, -1)
    d["wbf"] = np.ascontiguousarray(np.asarray(inp["w_br_fox"], f)[0].reshape(4, 128, D).transpose(1, 0, 2)).reshape(128, -1)
    d["wbd"] = np.ascontiguousarray(np.asarray(inp["w_br_dsa"], f)[0].reshape(4, 128, D).transpose(1, 0, 2)).reshape(128, -1)
    d["wo"] = np.ascontiguousarray(np.asarray(inp["w_out"], f)[0].reshape(KC, 128, D).transpose(1, 0, 2)).reshape(128, -1)
    d["fbias"] = np.ascontiguousarray(np.asarray(inp["fox_f_bias"], f)[0].reshape(1, 8))
    d["fqg"] = np.ascontiguousarray(np.asarray(inp["fox_qk_g"], f)[0].reshape(1, 128))
    d["dqg"] = np.ascontiguousarray(np.asarray(inp["dsa_qk_g"], f)[0].reshape(1, 128))
    return d


def _prep_core(inp, b, S):
    d = {}
    d["xT"] = np.ascontiguousarray(np.asarray(inp["x"], np.float32)[b, :S].T)
    d["cT"] = np.ascontiguousarray(np.asarray(inp["c"], np.float32)[b].reshape(KC, 128).T)
    d["pos"] = np.ascontiguousarray(np.asarray(inp["positions"], np.int32)[b, :S].reshape(S // 128, 128).T)
    return d


_NC_CACHE = {}


def kernel(**inputs):
    B, S, _ = inputs["x"].shape
    if S not in _NC_CACHE:
        _NC_CACHE[S] = build(S)
    nc = _NC_CACHE[S]
    shared = _prep_shared(inputs)
    in_maps = []
    for b in range(B):
        m = dict(shared)
        m.update(_prep_core(inputs, b, S))
        in_maps.append(m)
    res = run_bass_kernel_spmd(nc, in_maps, core_ids=list(range(B)))
    out = np.stack([np.ascontiguousarray(res.results[b]["outT"].T) for b in range(B)], 0)
    return out.astype(np.float32)
```

```python
import math
from contextlib import ExitStack
import numpy as np
import concourse.bass as bass
import concourse.mybir as mybir
from concourse.bass_utils import run_bass_kernel_spmd

F32 = mybir.dt.float32
BF16 = mybir.dt.bfloat16
I32 = mybir.dt.int32
AF = mybir.ActivationFunctionType
ALU = mybir.AluOpType
AX = mybir.AxisListType

D = 1024
DFF = 2816
NJ = 22
KC = 8
EPS = 1e-6
BIG = 1.0e30
NIT = 20
TOPK = 256
NH_DVE = 8
SB_BASE = 16640
SB_END = 229376
ENGS = ("pe", "act", "dve", "pool", "sp")


class Sched:
    def __init__(self, nc, es):
        self.nc = nc
        self.sem = {e: es.enter_context(nc.semaphore("s_" + e)) for e in ENGS}
        self.cnt = {e: 0 for e in ENGS}
        self.ops = {e: [] for e in ENGS}
        self.seen = {e: {} for e in ENGS}
        self.res = {}
        self.enabled = True
        self.dq = {}
        for q, n in (("sp", 24), ("pool", 16)):
            self.dq[q] = dict(sems=[es.enter_context(nc.semaphore("d_%s%d" % (q, i))) for i in range(n)],
                              tgt=[0] * n, nxt=0)

    def _wait(self, e, tok):
        if tok[0] == "e":
            _, e2, n = tok
            if e2 == e and e == "pe":
                return
            key = e2
        else:
            _, q, i, n = tok
            key = (q, i)
        if self.seen[e].get(key, 0) >= n:
            return
        self.seen[e][key] = n
        self.ops[e].append(("w", tok))

    def _deps(self, e, reads, writes):
        for r in reads:
            ent = self.res.get(r)
            if ent and ent[0]:
                self._wait(e, ent[0])
        for r in writes:
            ent = self.res.get(r)
            if ent:
                if ent[0]:
                    self._wait(e, ent[0])
                for t in ent[1]:
                    self._wait(e, t)

    def _commit(self, tok, reads, writes):
        src = tok[:2] if tok[0] == "e" else tok[:3]
        for r in writes:
            self.res[r] = [tok, []]
        for r in reads:
            ent = self.res.setdefault(r, [None, []])
            ent[1] = [t for t in ent[1] if (t[:2] if t[0] == "e" else t[:3]) != src] + [tok]

    def op(self, e, fn, reads=(), writes=()):
        if not self.enabled:
            return
        self._deps(e, reads, writes)
        self.cnt[e] += 1
        tok = ("e", e, self.cnt[e])
        self.ops[e].append(("o", fn))
        self._commit(tok, reads, writes)

    def dma(self, q, out, in_, reads=(), writes=()):
        if not self.enabled:
            return
        self._deps(q, reads, writes)
        d = self.dq[q]
        i = d["nxt"]
        d["nxt"] = (i + 1) % len(d["sems"])
        if d["tgt"][i] > 0:
            self._wait(q, ("d", q, i, d["tgt"][i]))
        d["tgt"][i] += 16
        tok = ("d", q, i, d["tgt"][i])
        self.ops[q].append(("d", out, in_, d["sems"][i]))
        self._commit(tok, reads, writes)

    def barrier(self):
        for e in ENGS:
            for e2 in ENGS:
                if self.cnt[e2] > 0:
                    self._wait(e, ("e", e2, self.cnt[e2]))
            for q, d in self.dq.items():
                for i, t in enumerate(d["tgt"]):
                    if t:
                        self._wait(e, ("d", q, i, t))
        self.res.clear()

    def emit(self, block):
        def run(e, h):
            for it in self.ops[e]:
                if it[0] == "w":
                    tok = it[1]
                    if tok[0] == "e":
                        h.wait_ge(self.sem[tok[1]], tok[2])
                    else:
                        h.wait_ge(self.dq[tok[1]]["sems"][tok[2]], tok[3])
                elif it[0] == "o":
                    it[1](h).then_inc(self.sem[e], 1)
                else:
                    h.dma_start(out=it[1], in_=it[2]).then_inc(it[3], 16)
        block.tensor(lambda h: run("pe", h))
        block.scalar(lambda h: run("act", h))
        block.vector(lambda h: run("dve", h))
        block.gpsimd(lambda h: run("pool", h))
        block.sync(lambda h: run("sp", h))

    def mm(self, out, lhsT, rhs, start, stop, r, w):
        self.op("pe", lambda h: h.matmul(out, lhsT, rhs, start=start, stop=stop), r, w)

    def tr(self, out, in_, ident, r, w):
        self.op("pe", lambda h: h.transpose(out, in_, ident), r, w)

    def act(self, out, in_, func, r, w, bias=0.0, scale=1.0):
        self.op("act", lambda h: h.activation(out, in_, func, bias=bias, scale=scale), r, w)

    def tt(self, e, out, a, b, op, r, w):
        self.op(e, lambda h: h.tensor_tensor(out, a, b, op), r, w)

    def ts(self, e, out, a, s1, s2, op0, op1, r, w, accum=None):
        if accum is None:
            self.op(e, lambda h: h.tensor_scalar(out, a, s1, s2, op0, op1), r, w)
        else:
            self.op(e, lambda h: h.tensor_scalar(out, a, s1, s2, op0, op1, accum_out=accum), r, w)

    def stt(self, e, out, in0, scalar, in1, op0, op1, r, w):
        self.op(e, lambda h: h.scalar_tensor_tensor(out, in0, scalar, in1, op0, op1), r, w)

    def cp(self, e, out, in_, r, w):
        if e == "act":
            self.op(e, lambda h: h.copy(out, in_), r, w)
        else:
            self.op(e, lambda h: h.tensor_copy(out, in_), r, w)

    def memset(self, e, ap, val, w):
        self.op(e, lambda h: h.memset(ap, val), (), w)

    def recip(self, out, in_, r, w):
        self.op("dve", lambda h: h.reciprocal(out, in_), r, w)


class Alloc:
    def __init__(self, nc):
        self.nc = nc
        self.off = SB_BASE
        self.n = 0

    def __call__(self, name, shape, dt):
        nbytes = int(np.prod(shape[1:])) * (4 if dt in (F32, I32) else 2)
        nbytes = (nbytes + 31) // 32 * 32
        assert self.off + nbytes <= SB_END, "SBUF overflow at %s: %d" % (name, self.off + nbytes)
        self.n += 1
        t = self.nc.alloc_sbuf_tensor_at("%s_%d" % (name, self.n), list(shape), dt, offset=self.off)
        self.off += nbytes
        return t

    def mark(self):
        return self.off

    def release(self, m):
        self.off = m


def _view(ap2d, shape):
    if len(shape) == 1:
        return ap2d
    if len(shape) == 2:
        return ap2d.rearrange("p (a b) -> p a b", a=shape[0])
    if len(shape) == 3:
        return ap2d.rearrange("p (a b c) -> p a b c", a=shape[0], b=shape[1])
    raise ValueError(shape)


def build(S, debug=False, phases="0AB1B2C"):
    NT = S // 128
    NB = S // 512
    nc = bass.Bass("TRN2", target_bir_lowering=False)
    es = ExitStack()

    def din(name, shape, dt=F32):
        return nc.dram_tensor(name, list(shape), dt, kind="ExternalInput").ap()

    def dscr(name, shape, dt):
        kind = "ExternalOutput" if (debug and name in DEBUG_OUT) else "Internal"
        return nc.dram_tensor(name, list(shape), dt, kind=kind).ap()

    DEBUG_OUT = ("X1T", "FQT", "FKT", "FV", "DQT", "DKT", "DV", "IQT", "IKT", "YAT", "YBT", "SGT", "DBG")

    xT = din("xT", [D, S])
    cT = din("cT", [128, KC])
    pos = din("pos", [128, NT], I32)
    ada_w = din("ada_w", [128, KC, 9 * D])
    ada_bT = din("ada_bT", [128, 72])
    norm_gT = din("norm_gT", [128, 3, KC])
    wf = {}
    for p in ("f1", "f2"):
        wf[p + "w1"] = din(p + "w1", [128, NJ * KC * 128])
        wf[p + "w3"] = din(p + "w3", [128, NJ * KC * 128])
        wf[p + "w2"] = din(p + "w2", [128, KC * NJ * 128])
    wf["wt"] = din("wt", [128, 5 * KC * 512])
    wf["wg"] = din("wg", [128, 16 * KC * 128])
    wf["wbf"] = din("wbf", [128, 4 * D])
    wf["wbd"] = din("wbd", [128, 4 * D])
    wf["wo"] = din("wo", [128, KC * D])
    fbias = din("fbias", [1, 8])
    fqg = din("fqg", [1, 128])
    dqg = din("dqg", [1, 128])
    outT = nc.dram_tensor("outT", [D, S], F32, kind="ExternalOutput").ap()

    wb = {k: dscr("b_" + k, list(v.shape), BF16) for k, v in wf.items()}
    X1T = dscr("X1T", [D, S], F32)
    SGT = dscr("SGT", [2 * D, S], BF16)
    FQT = dscr("FQT", [512, S], BF16)
    FKT = dscr("FKT", [512, S], BF16)
    FV = dscr("FV", [S, 512], BF16)
    DQT = dscr("DQT", [512, S], BF16)
    DKT = dscr("DKT", [64, S], BF16)
    DV = dscr("DV", [S, 64], BF16)
    IQT = dscr("IQT", [256, S], BF16)
    IKT = dscr("IKT", [32, S], BF16)
    YAT = dscr("YAT", [512, S], BF16)
    YBT = dscr("YBT", [512, S], BF16)
    DBG = dscr("DBG", [128, 4096], F32) if debug else None

    Sx = Sched(nc, es)
    al = Alloc(nc)
    PS = [es.enter_context(nc.psum_tensor("ps%d" % i, [128, 512], F32)) for i in range(8)]

    def psb(i):
        return ("ps", i)

    def ps_bf(i):
        return PS[i][:].bitcast(BF16)

    ident = al("ident", [128, 128], BF16)
    identf = al("identf", [128, 128], F32)
    onesf = al("onesf", [128, 128], F32)
    Umat = al("Umat", [128, 128], F32)
    modT = al("modT", [128, 72], F32)
    coefA = al("coefA", [128, 3, KC], F32)
    coefG = al("coefG", [128, 3, KC], F32)
    cosA = al("cosA", [128, NT, 8], F32)
    sinA = al("sinA", [128, NT, 8], F32)
    cosI = al("cosI", [128, NT, 4], F32)
    sinI = al("sinI", [128, NT, 4], F32)
    Fk = al("Fk", [128, NT, 8], F32)
    Fprev = al("Fprev", [128, NT + 1, 8], F32)
    LO = al("LO", [128, NT, 8], F32)
    HI = al("HI", [128, NT, 8], F32)
    fb_bc = al("fb_bc", [128, 8], F32)
    fqg_bc = al("fqg_bc", [128, 128], F32)
    dqg_bc = al("dqg_bc", [128, 128], F32)
    pers_mark = al.mark()

    Sx.enabled = "0" in phases
    cast_order = ["f1w1", "f1w3", "f1w2", "wg", "wt", "wbf", "wbd", "wo", "f2w1", "f2w3", "f2w2"]
    for k in cast_order:
        n = wf[k].shape[1]
        step = 8192
        for c0 in range(0, n, step):
            c1 = min(n, c0 + step)
            Sx.dma("pool", wb[k][:, c0:c1], wf[k][:, c0:c1], (), [("W", k)] if c1 == n else [("Wpart", k, c0)])

    def w_ready(k):
        n = wf[k].shape[1]
        return [("W", k)] + [("Wpart", k, c0) for c0 in range(0, n, 8192) if min(n, c0 + 8192) != n]

    Sx.memset("pool", identf[:], 1.0, ["identf"])
    Sx.op("pool", lambda h: h.affine_select(out=identf[:], in_=identf[:], pattern=[[-1, 128]], compare_op=ALU.is_equal,
                                            fill=0.0, base=0, channel_multiplier=1), ["identf"], ["identf"])
    Sx.cp("dve", ident[:], identf[:], ["identf"], ["ident"])
    Sx.memset("dve", onesf[:], 1.0, ["onesf"])
    Sx.memset("pool", Umat[:], 1.0, ["Umat"])
    Sx.op("pool", lambda h: h.affine_select(out=Umat[:], in_=Umat[:], pattern=[[1, 128]], compare_op=ALU.is_ge,
                                            fill=0.0, base=0, channel_multiplier=-1), ["Umat"], ["Umat"])
    Sx.memset("dve", Fprev[:, 0, :], 0.0, ["Fprev"])
    Sx.dma("sp", fb_bc[:], fbias.partition_broadcast(128), (), ["fb_bc"])
    Sx.dma("sp", fqg_bc[:], fqg.partition_broadcast(128), (), ["fqg_bc"])
    Sx.dma("sp", dqg_bc[:], dqg.partition_broadcast(128), (), ["dqg_bc"])

    m0 = al.mark()
    cond = al("cond", [128, KC], F32)
    ngT = al("ngT", [128, 3, KC], F32)
    abT = al("abT", [128, 72], F32)
    posi = al("posi", [128, NT], I32)
    posf = al("posf", [128, NT], F32)
    adaw = [al("adaw%d" % i, [128, KC, 512], F32) for i in range(2)]
    Sx.dma("sp", cond[:], cT, (), ["cond"])
    Sx.dma("sp", ngT[:], norm_gT, (), ["ngT"])
    Sx.dma("sp", abT[:], ada_bT, (), ["abT"])
    Sx.dma("sp", posi[:], pos, (), ["posi"])
    Sx.act(cond[:], cond[:], AF.Silu, ["cond"], ["cond"])
    for g in range(18):
        sl = adaw[g % 2]
        Sx.dma("sp", sl[:], ada_w[:, :, g * 512:(g + 1) * 512], (), [("adaw", g % 2)])
        for cc in range(4):
            c = g * 4 + cc
            for kc in range(KC):
                Sx.mm(PS[0][:, c:c + 1], sl[:, kc, cc * 128:(cc + 1) * 128], cond[:, kc:kc + 1], kc == 0, kc == KC - 1,
                      [("adaw", g % 2), "cond"], [psb(0)])
    Sx.tt("dve", modT[:], PS[0][:, 0:72], abT[:], ALU.add, [psb(0), "abT"], ["modT"])
    for i in range(3):
        Sx.stt("dve", coefA[:, i, :], modT[:, (i * 3 + 1) * 8:(i * 3 + 2) * 8], 1.0, ngT[:, i, :], ALU.add, ALU.mult,
               ["modT", "ngT"], ["coef"])
        Sx.ts("dve", coefG[:, i, :], modT[:, (i * 3 + 2) * 8:(i * 3 + 3) * 8], 0.5 if i != 1 else 1.0, 0.0, ALU.mult, ALU.add,
              ["modT"], ["coef"])

    def shiftcol(i, kc):
        return modT[:, i * 24 + kc:i * 24 + kc + 1]

    Sx.cp("dve", posf[:], posi[:], ["posi"], ["posf"])
    TWO_PI = 2.0 * math.pi
    C1 = 6.28125
    _c2 = np.array([TWO_PI - C1], dtype=np.float32)
    _c2 = (_c2.view(np.uint32) & np.uint32(0xFFFFE000)).view(np.float32)
    C2 = float(_c2[0])
    C3 = float(TWO_PI - C1 - C2)
    PI_LO = 3.1415925
    for (nf, rot, ctab, stab) in ((8, 16, cosA, sinA), (4, 8, cosI, sinI)):
        mr = al.mark()
        ang = al("ang", [128, NT, nf], F32)
        kf = al("kf", [128, NT * nf], F32)
        ki = al("ki", [128, NT * nf], I32)
        rr = al("rr", [128, NT * nf], F32)
        t1 = al("t1", [128, NT * nf], F32)
        inv = (np.float32(500000.0) ** (-np.arange(0, rot, 2, dtype=np.float32) / np.float32(rot))).astype(np.float32)
        for j in range(nf):
            Sx.ts("dve", ang[:, :, j], posf[:], float(inv[j]), 0.0, ALU.mult, ALU.add, ["posf"], ["ang"])
        a2 = ang[:].rearrange("p a b -> p (a b)")
        Sx.ts("dve", kf[:], a2, 1.0 / TWO_PI, 0.0, ALU.mult, ALU.add, ["ang"], ["kf"])
        Sx.cp("dve", ki[:], kf[:], ["kf"], ["ki"])
        Sx.cp("dve", kf[:], ki[:], ["ki"], ["kf"])
        Sx.stt("dve", rr[:], kf[:], -C1, a2, ALU.mult, ALU.add, ["kf", "ang"], ["rr"])
        Sx.stt("dve", rr[:], kf[:], -C2, rr[:], ALU.mult, ALU.add, ["kf", "rr"], ["rr"])
        Sx.stt("dve", rr[:], kf[:], -C3, rr[:], ALU.mult, ALU.add, ["kf", "rr"], ["rr"])
        Sx.ts("dve", t1[:], rr[:], math.pi, -TWO_PI, ALU.is_gt, ALU.mult, ["rr"], ["t1"])
        Sx.tt("dve", rr[:], rr[:], t1[:], ALU.add, ["rr", "t1"], ["rr"])
        Sx.ts("dve", t1[:], rr[:], -math.pi, TWO_PI, ALU.is_lt, ALU.mult, ["rr"], ["t1"])
        Sx.tt("dve", rr[:], rr[:], t1[:], ALU.add, ["rr", "t1"], ["rr"])
        Sx.ts("dve", kf[:], rr[:], -PI_LO, PI_LO, ALU.max, ALU.min, ["rr"], ["kf"])
        Sx.act(stab[:].rearrange("p a b -> p (a b)"), kf[:], AF.Sin, ["kf"], ["rope"])
        Sx.ts("dve", rr[:], rr[:], math.pi / 2, 0.0, ALU.add, ALU.add, ["rr"], ["rr"])
        Sx.ts("dve", t1[:], rr[:], math.pi, -TWO_PI, ALU.is_gt, ALU.mult, ["rr"], ["t1"])
        Sx.tt("dve", rr[:], rr[:], t1[:], ALU.add, ["rr", "t1"], ["rr"])
        Sx.ts("dve", t1[:], rr[:], -PI_LO, PI_LO, ALU.max, ALU.min, ["rr"], ["t1"])
        Sx.act(ctab[:].rearrange("p a b -> p (a b)"), t1[:], AF.Sin, ["t1"], ["rope"])
        Sx.barrier()
        al.release(mr)
    Sx.barrier()
    al.release(m0)

    class Ring:
        def __init__(self, name, nslots, nel):
            self.name = name
            self.tiles = [al("%s%d" % (name, i), [128, nel], BF16) for i in range(nslots)]
            self.k = 0

        def load(self, key, c0, shape):
            i = self.k % len(self.tiles)
            self.k += 1
            nel = int(np.prod(shape))
            v2 = self.tiles[i][:, 0:nel]
            Sx.dma("sp", v2, wb[key][:, c0:c0 + nel], w_ready(key), [(self.name, i)])
            return _view(v2, shape), (self.name, i)

    def norm_mod(i, xT_sb, hT_sb, sq, tmp, rs, rstd):
        for kc in range(KC):
            Sx.act(sq[kc % 2][:], xT_sb[:, kc, :], AF.Square, [("xT", kc)], [("sq", kc % 2)])
            Sx.mm(PS[7][:], onesf[:], sq[kc % 2][:], kc == 0, kc == KC - 1, [("sq", kc % 2), "onesf"], [psb(7)])
        Sx.act(rs[:], PS[7][:], AF.Sqrt, [psb(7)], ["rs"], bias=EPS, scale=1.0 / D)
        Sx.recip(rstd[:], rs[:], ["rs"], ["rstd"])
        for kc in range(KC):
            Sx.stt("dve", tmp[kc % 2][:], xT_sb[:, kc, :], coefA[:, i, kc:kc + 1], rstd[:], ALU.mult, ALU.mult,
                   [("xT", kc), "rstd", "coef"], [("tmp", kc % 2)])
            Sx.act(hT_sb[:, kc, :], tmp[kc % 2][:], AF.Identity, [("tmp", kc % 2), "modT"], [("hT", kc)],
                   bias=shiftcol(i, kc), scale=1.0)

    def ffn(i, pfx, ring, xT_sb, hT_sb, aT, sil):
        hreads = [("hT", kc) for kc in range(KC)]
        for j0 in range(0, NJ, 4):
            nj = min(4, NJ - j0)
            w1p, r1 = ring.load(pfx + "w1", j0 * KC * 128, [nj, KC, 128])
            w3p, r3 = ring.load(pfx + "w3", j0 * KC * 128, [nj, KC, 128])
            for jj in range(nj):
                j = j0 + jj
                ub, vb = (j % 2), 2 + (j % 2)
                for kc in range(KC):
                    Sx.mm(PS[ub][:], w1p[:, jj, kc, :], hT_sb[:, kc, :], kc == 0, kc == KC - 1, [r1, ("hT", kc)], [psb(ub)])
                for kc in range(KC):
                    Sx.mm(PS[vb][:], w3p[:, jj, kc, :], hT_sb[:, kc, :], kc == 0, kc == KC - 1, [r3, ("hT", kc)], [psb(vb)])
                Sx.act(sil[j % 2][:], PS[ub][:], AF.Silu, [psb(ub)], [("sil", j % 2)])
                Sx.tt("dve", aT[:, j, :], sil[j % 2][:], PS[vb][:], ALU.mult, [("sil", j % 2), psb(vb)], [("aT", j)])
        for m in range(KC):
            w2p, r2 = ring.load(pfx + "w2", m * NJ * 128, [NJ, 128])
            yb = 4 + (m % 2)
            for j in range(NJ):
                Sx.mm(PS[yb][:], w2p[:, j, :], aT[:, j, :], j == 0, j == NJ - 1, [r2, ("aT", j)], [psb(yb)])
            Sx.stt("dve", xT_sb[:, m, :], PS[yb][:], coefG[:, i, m:m + 1], xT_sb[:, m, :], ALU.mult, ALU.add,
                   [psb(yb), ("xT", m), "coef"], [("xT", m)])

    Sx.enabled = "A" in phases
    mA = al.mark()
    ring = Ring("wr", 6, 4096)
    xT_sb = al("xT_sb", [128, KC, 512], F32)
    hT_sb = al("hT_sb", [128, KC, 512], BF16)
    sq = [al("sq%d" % i, [128, 512], F32) for i in range(2)]
    tmp = [al("tmp%d" % i, [128, 512], F32) for i in range(2)]
    rs = al("rs", [128, 512], F32)
    rstd = al("rstd", [128, 512], F32)
    aT = al("aT", [128, NJ, 512], BF16)
    sil = [al("sil%d" % i, [128, 512], F32) for i in range(2)]
    sg = al("sg", [128, 16, 512], BF16)
    zq2 = [al("zq%d" % i, [128, 512], F32) for i in range(2)]
    zn2 = [al("zn%d" % i, [128, 512], F32) for i in range(2)]
    zr2 = [al("zr%d" % i, [128, 512], F32) for i in range(2)]
    qb2 = [al("qb%d" % i, [128, 512], BF16) for i in range(2)]
    ssq2 = [al("ssq%d" % i, [128, 8], F32) for i in range(2)]
    rq2 = [al("rq%d" % i, [128, 8], F32) for i in range(2)]
    rt2 = [[al("rt%d_%d" % (i, k), [128, 64], F32) for i in range(4)] for k in range(2)]
    PAR = {"p": 0}
    qTst = al("qTst", [128, 4, 512], BF16)
    vst = al("vst", [128, 4, 512], BF16)
    zm = al("zm", [128, 432], F32)
    lf = al("lf", [128, 8], F32)
    lf2 = al("lf2", [128, 8], F32)
    wsc = al("wsc", [128, 8], F32)
    dkb = al("dkb", [128, 64], BF16)
    iqb = al("iqb", [128, 256], BF16)
    iqf = al("iqf", [128, 256], F32)
    ikb = al("ikb", [128, 32], BF16)
    ikf = al("ikf", [128, 32], F32)
    dkst = al("dkst", [64, 512], BF16)
    iqst = al("iqst", [128, 2, 512], BF16)
    ikst = al("ikst", [32, 512], BF16)
    dvst = al("dvst", [128, 4, 64], BF16)

    X1Tv = X1T.rearrange("(kc p) t -> p kc t", p=128)
    xTv = xT.rearrange("(kc p) t -> p kc t", p=128)
    outTv = outT.rearrange("(kc p) t -> p kc t", p=128)
    SGTv = SGT.rearrange("(c p) t -> p c t", p=128)
    xres = [("xT", kc) for kc in range(KC)]
    hres = [("hT", kc) for kc in range(KC)]
    WSCALE = (8 ** -0.5) * (32 ** -0.5)

    def qknorm(src_ps_ap, nh, gain_ap, dst, gdst_res, resr):
        n = nh * 64
        p = PAR["p"]
        zq, zn, ssq, rq = zq2[p], zn2[p], ssq2[p], rq2[p]
        Sx.act(zq[:, 0:n], src_ps_ap, AF.Square, resr, [("zq", p)])
        Sx.op("dve", lambda h: h.tensor_reduce(ssq[:, 0:nh], zq[:, 0:n].rearrange("p (h d) -> p h d", h=nh), AX.X, ALU.add),
              [("zq", p)], [("ssq", p)])
        Sx.act(rq[:, 0:nh], ssq[:, 0:nh], AF.Sqrt, [("ssq", p)], [("rq", p)], bias=EPS, scale=1.0 / 64)
        Sx.recip(rq[:, 0:nh], rq[:, 0:nh], [("rq", p)], [("rq", p)])
        Sx.tt("dve", zn[:, 0:n].rearrange("p (h d) -> p h d", h=nh), src_ps_ap.rearrange("p (h d) -> p h d", h=nh),
              rq[:, 0:nh].unsqueeze(2).to_broadcast([128, nh, 64]), ALU.mult, resr + [("rq", p)], [("zn", p)])
        Sx.tt("dve", dst.rearrange("p (h d) -> p h d", h=nh), zn[:, 0:n].rearrange("p (h d) -> p h d", h=nh),
              gain_ap.unsqueeze(1).to_broadcast([128, nh, 64]), ALU.mult, [("zn", p), "gains"], gdst_res)

    def rope(src, dst, nh, dh, half, ctab, stab, tile, resr, resw):
        p = PAR["p"]
        rt = rt2[p]
        s3 = src.rearrange("p (h d) -> p h d", h=nh)
        d3 = dst.rearrange("p (h d) -> p h d", h=nh)
        c = ctab[:, tile, :].unsqueeze(1).to_broadcast([128, nh, half])
        s = stab[:, tile, :].unsqueeze(1).to_broadcast([128, nh, half])
        x1 = s3[:, :, 0:half]
        x2 = s3[:, :, half:2 * half]
        tv = [rt[k][:, 0:nh * half].rearrange("p (h d) -> p h d", h=nh) for k in range(4)]
        Sx.tt("dve", tv[0], x1, c, ALU.mult, resr + ["rope"], [("rt0", p)])
        Sx.tt("dve", tv[1], x2, s, ALU.mult, resr + ["rope"], [("rt1", p)])
        Sx.tt("dve", tv[2], x2, c, ALU.mult, resr + ["rope"], [("rt2", p)])
        Sx.tt("dve", tv[3], x1, s, ALU.mult, resr + ["rope"], [("rt3", p)])
        Sx.tt("dve", d3[:, :, 0:half], tv[0], tv[1], ALU.subtract, [("rt0", p), ("rt1", p)], resw)
        Sx.tt("dve", d3[:, :, half:2 * half], tv[2], tv[3], ALU.add, [("rt2", p), ("rt3", p)], resw)
        Sx.cp("dve", d3[:, :, 2 * half:dh], s3[:, :, 2 * half:dh], resr, resw)

    for tb in range(NB):
        t0 = tb * 512
        Sx.dma("sp", xT_sb[:], xTv[:, :, t0:t0 + 512], (), xres)
        norm_mod(0, xT_sb, hT_sb, sq, tmp, rs, rstd)
        ffn(0, "f1", ring, xT_sb, hT_sb, aT, sil)
        Sx.dma("pool", X1Tv[:, :, t0:t0 + 512], xT_sb[:], xres, [("X1T", tb)])
        norm_mod(1, xT_sb, hT_sb, sq, tmp, rs, rstd)
        for cg in range(4):
            wgp, rg = ring.load("wg", cg * 4 * KC * 128, [4, KC, 128])
            for cc in range(4):
                c = cg * 4 + cc
                gbk = c % 2
                for kc in range(KC):
                    Sx.mm(PS[gbk][:], wgp[:, cc, kc, :], hT_sb[:, kc, :], kc == 0, kc == KC - 1, [rg, ("hT", kc)], [psb(gbk)])
                Sx.act(sg[:, c, :], PS[gbk][:], AF.Sigmoid, [psb(gbk)], [("sg", c)])
        Sx.dma("pool", SGTv[:, :, t0:t0 + 512], sg[:], [("sg", c) for c in range(16)], [("SGT", tb)])
        for g in range(5):
            wtp, rw = ring.load("wt", g * KC * 512, [KC, 512])
            ncol = 512 if g < 4 else 432
            for s in range(4):
                tile = tb * 4 + s
                zb = (g * 4 + s) % 4
                tb_ = 4 + (g * 4 + s) % 3
                for kc in range(KC):
                    Sx.mm(PS[zb][:, 0:ncol], hT_sb[:, kc, s * 128:(s + 1) * 128], wtp[:, kc, 0:ncol], kc == 0, kc == KC - 1,
                          [rw, ("hT", kc)], [psb(zb)])
                z = PS[zb]
                if g in (0, 1, 3):
                    gain = (fqg_bc[:, 0:64], fqg_bc[:, 64:128], None, dqg_bc[:, 0:64])[g]
                    PAR["p"] = (g * 4 + s) % 2
                    pp = PAR["p"]
                    qb, zr = qb2[pp], zr2[pp]
                    if g == 3:
                        qknorm(z[:, 0:512], 8, gain, zr[:, 0:512], [("zr", pp)], [psb(zb)])
                        rope(zr[:, 0:512], qb[:, 0:512], 8, 64, 8, cosA, sinA, tile, [("zr", pp)], [("qb", pp)])
                    else:
                        qknorm(z[:, 0:512], 8, gain, qb[:, 0:512], [("qb", pp)], [psb(zb)])
                    for c in range(4):
                        Sx.tr(ps_bf(tb_)[:, c * 128:(c + 1) * 128], qb[:, c * 128:(c + 1) * 128], ident[:], [("qb", pp), "ident"], [psb(tb_)])
                    Sx.cp("act", qTst[:, :, s * 128:(s + 1) * 128], ps_bf(tb_)[:, 0:512].rearrange("p (c t) -> p c t", c=4),
                          [psb(tb_)], [("qTst", s)])
                elif g == 2:
                    Sx.cp("act", vst[:, s, :], z[:, 0:512], [psb(zb)], [("vst", s)])
                else:
                    Sx.cp("act", zm[:], z[:, 0:432], [psb(zb)], ["zm"])
                    Sx.tt("dve", lf[:], zm[:, 0:8], fb_bc[:], ALU.add, ["zm", "fb_bc"], ["lf"])
                    Sx.act(lf[:], lf[:], AF.Exp, ["lf"], ["lf"], scale=-1.0)
                    Sx.act(lf[:], lf[:], AF.Ln, ["lf"], ["lf"], bias=1.0)
                    Sx.ts("dve", lf2[:], lf[:], -1.0, 0.0, ALU.mult, ALU.add, ["lf"], ["lf2"])
                    Sx.mm(PS[7][:, 0:8], Umat[:], lf2[:], True, True, ["lf2", "Umat"], [psb(7)])
                    Sx.mm(PS[7][:, 8:16], onesf[:], lf2[:], True, True, ["lf2", "onesf"], [psb(7)])
                    Sx.tt("dve", Fk[:, tile, :], PS[7][:, 0:8], Fprev[:, tile, :], ALU.add, [psb(7), "Fprev"], ["Fk"])
                    Sx.tt("dve", Fprev[:, tile + 1, :], PS[7][:, 8:16], Fprev[:, tile, :], ALU.add, [psb(7), "Fprev"], ["Fprev"])
                    PAR["p"] = s % 2
                    zr = zr2[s % 2]
                    qknorm(zm[:, 8:72], 1, dqg_bc[:, 64:128], zr[:, 0:64], [("zr", s % 2)], ["zm"])
                    rope(zr[:, 0:64], dkb[:], 1, 64, 8, cosA, sinA, tile, [("zr", s % 2)], ["dkb"])
                    Sx.cp("dve", dvst[:, s, :], zm[:, 72:136], ["zm"], [("dvst", s)])
                    Sx.ts("dve", wsc[:], zm[:, 424:432], WSCALE, 0.0, ALU.mult, ALU.add, ["zm"], ["wsc"])
                    Sx.stt("dve", LO[:, tile, :], wsc[:], -1.0, wsc[:], ALU.mult, ALU.max, ["wsc"], ["LOHI"])
                    Sx.ts("dve", HI[:, tile, :], wsc[:], 0.0, 2.0, ALU.is_ge, ALU.mult, ["wsc"], ["LOHI"])
                    Sx.ts("dve", HI[:, tile, :], HI[:, tile, :], -1.0, 0.0, ALU.add, ALU.add, ["LOHI"], ["LOHI"])
                    rope(zm[:, 136:392], iqf[:], 8, 32, 4, cosI, sinI, tile, ["zm"], ["iqf"])
                    Sx.cp("dve", iqb[:], iqf[:], ["iqf"], ["iqb"])
                    rope(zm[:, 392:424], ikf[:], 1, 32, 4, cosI, sinI, tile, ["zm"], ["ikf"])
                    Sx.cp("dve", ikb[:], ikf[:], ["ikf"], ["ikb"])
                    pb = ps_bf(tb_)
                    Sx.tr(pb[0:64, 0:128], dkb[:], ident[:], ["dkb", "ident"], [psb(tb_)])
                    Sx.tr(pb[:, 128:256], iqb[:, 0:128], ident[:], ["iqb", "ident"], [psb(tb_)])
                    Sx.tr(pb[:, 256:384], iqb[:, 128:256], ident[:], ["iqb", "ident"], [psb(tb_)])
                    Sx.tr(pb[0:32, 384:512], ikb[:], ident[:], ["ikb", "ident"], [psb(tb_)])
                    Sx.cp("act", dkst[:, s * 128:(s + 1) * 128], pb[0:64, 0:128], [psb(tb_)], [("dkst", s)])
                    Sx.cp("act", iqst[:, :, s * 128:(s + 1) * 128], pb[:, 128:384].rearrange("p (c t) -> p c t", c=2),
                          [psb(tb_)], [("iqst", s)])
                    Sx.cp("act", ikst[:, s * 128:(s + 1) * 128], pb[0:32, 384:512], [psb(tb_)], [("ikst", s)])
            if g in (0, 1, 3):
                dst = (FQT, FKT, None, DQT)[g]
                Sx.dma("pool", dst.rearrange("(c p) t -> p c t", p=128)[:, :, t0:t0 + 512], qTst[:],
                       [("qTst", s) for s in range(4)], [("QKT", g, tb)])
            elif g == 2:
                Sx.dma("pool", FV.rearrange("(s p) c -> p s c", p=128)[:, tb * 4:(tb + 1) * 4, :], vst[:],
                       [("vst", s) for s in range(4)], [("FV", tb)])
            else:
                Sx.dma("pool", DKT[:, t0:t0 + 512], dkst[:], [("dkst", s) for s in range(4)], [("DKT", tb)])
                Sx.dma("pool", IQT.rearrange("(c p) t -> p c t", p=128)[:, :, t0:t0 + 512], iqst[:],
                       [("iqst", s) for s in range(4)], [("IQT", tb)])
                Sx.dma("pool", IKT[:, t0:t0 + 512], ikst[:], [("ikst", s) for s in range(4)], [("IKT", tb)])
                Sx.dma("pool", DV.rearrange("(s p) c -> p s c", p=128)[:, tb * 4:(tb + 1) * 4, :], dvst[:],
                       [("dvst", s) for s in range(4)], [("DV", tb)])
    Sx.barrier()
    al.release(mA)

    Sx.enabled = "B1" in phases
    mB = al.mark()
    KT = al("KT", [128, S], BF16)
    QT = al("QT", [128, S], BF16)
    Vaug = al("Vaug", [128, NT, 2, 128], BF16)
    QB = 256
    NQ = S // QB
    NDG = QB // 128
    CM = [al("CM%d" % j, [128, QB], BF16) for j in range(NDG)]
    bq = [al("bq%d" % i, [128, NT], F32) for i in range(2)]
    pT = [[al("pT%d_%d" % (h2, i), [128, QB], BF16) for i in range(3)] for h2 in range(2)]
    rc = al("rc", [128, QB], F32)
    yo = [[al("yo%d_%d" % (h2, i), [128, QB], BF16) for i in range(2)] for h2 in range(2)]
    Sx.memset("pool", Vaug[:, :, :, 64:128], 1.0, ["Vaug1"])
    for j in range(NDG):
        Sx.memset("pool", CM[j][:], 1.0, [("CM", j)])
        Sx.op("pool", (lambda jj: (lambda h: h.affine_select(out=CM[jj][:], in_=CM[jj][:], pattern=[[1, QB]],
                                                              compare_op=ALU.is_ge, fill=0.0, base=-128 * jj,
                                                              channel_multiplier=-1)))(j), [("CM", j)], [("CM", j)])
    FVv = FV.rearrange("(kb p) c -> p kb c", p=128)
    for hp in range(4):
        Sx.dma("sp", KT[:], FKT[hp * 128:(hp + 1) * 128, :], (), ["KT"])
        Sx.dma("sp", QT[:], FQT[hp * 128:(hp + 1) * 128, :], (), ["QT"])
        for h2 in range(2):
            Sx.dma("sp", Vaug[:, :, h2, 0:64], FVv[:, :, hp * 128 + h2 * 64:hp * 128 + h2 * 64 + 64], (), [("Vaug", h2)])
        for Q in range(NQ):
            nkb = NDG * (Q + 1)
            obs = [4 + 2 * (Q % 2), 5 + 2 * (Q % 2)]
            for h2 in range(2):
                h = hp * 2 + h2
                Sx.ts("dve", bq[h2][:, 0:nkb], Fk[:, 0:nkb, h], Fprev[:, NDG * Q + NDG // 2, h:h + 1], -1.0, ALU.subtract, ALU.mult,
                      ["Fk", "Fprev"], [("bq", h2)])

            def qk(kb):
                for h2 in range(2):
                    b0 = h2 * 64
                    sbk = (kb % 2) * 2 + h2
                    Sx.mm(PS[sbk][:, 0:QB], KT[b0:b0 + 64, kb * 128:(kb + 1) * 128],
                          QT[b0:b0 + 64, Q * QB:(Q + 1) * QB], True, True, ["KT", "QT"], [psb(sbk)])

            def rest(kb):
                for h2 in range(2):
                    p_ = pT[h2][kb % 3]
                    sbk = (kb % 2) * 2 + h2
                    Sx.act(p_[:], PS[sbk][:, 0:QB], AF.Exp, [psb(sbk), ("bq", h2)], [("pT", h2, kb % 3)],
                           bias=bq[h2][:, kb:kb + 1], scale=0.125)
                    if kb >= NDG * Q:
                        Sx.tt("pool", p_[:], p_[:], CM[kb - NDG * Q][:], ALU.mult,
                              [("pT", h2, kb % 3), ("CM", kb - NDG * Q)], [("pT", h2, kb % 3)])
                for h2 in range(2):
                    Sx.mm(PS[obs[h2]][:, 0:QB], Vaug[:, kb, h2, :], pT[h2][kb % 3][:], kb == 0, kb == nkb - 1,
                          [("pT", h2, kb % 3), ("Vaug", h2), "Vaug1"], [psb(obs[h2])])

            qk(0)
            for kb in range(nkb):
                if kb + 1 < nkb:
                    qk(kb + 1)
                rest(kb)
            for h2 in range(2):
                h = hp * 2 + h2
                ob = obs[h2]
                Sx.recip(rc[64:128, :], PS[ob][64:128, 0:QB], [psb(ob)], ["rc"])
                Sx.tt("dve", yo[h2][Q % 2][0:64, :], PS[ob][0:64, 0:QB], rc[64:128, :], ALU.mult, [psb(ob), "rc"], [("yo", h2, Q % 2)])
                Sx.dma("pool", YAT[h * 64:(h + 1) * 64, Q * QB:(Q + 1) * QB], yo[h2][Q % 2][0:64, :], [("yo", h2, Q % 2)], [("YAT", h, Q)])
    Sx.barrier()
    al.release(mB)

    Sx.enabled = "B2" in phases
    mD = al.mark()
    dkT = al("dkT", [64, S], BF16)
    ikT = al("ikT", [32, S], BF16)
    dvaug = al("dvaug", [128, NT, 128], BF16)
    score = al("score", [128, S], F32)
    maskb = [al("maskb%d" % i, [128, S], BF16) for i in range(2)]
    maskT = al("maskT", [128, NT, 128], BF16)
    iqTi = [al("iqTi%d" % i, [32, 8, 128], BF16) for i in range(2)]
    dqTi = [al("dqTi%d" % i, [64, 1024], BF16) for i in range(2)]
    tmpc = [al("tmpc%d" % i, [128, 512], F32) for i in range(4)]
    sc2 = [al("sc2_%d" % i, [128, 512], F32) for i in range(2)]
    tmp2 = [al("tmp2_%d" % i, [128, 512], F32) for i in range(2)]
    NEGD = al("NEGD", [128, 128], F32)
    pow2 = al("pow2", [128, NIT], F32)
    am = al("am", [128, 1], F32)
    step = al("step", [128, NIT], F32)
    lo = al("lo", [128, NIT + 1], F32)
    tpt = al("tpt", [128, NIT + 1], F32)
    cnt = al("cnt", [128, NIT], F32)
    ge = al("ge", [128, NIT], F32)
    pTd = [al("pTd%d" % i, [128, 1024], BF16) for i in range(2)]
    rc2 = al("rc2", [128, 512], F32)
    ybo = [al("ybo%d" % i, [128, 1024], BF16) for i in range(2)]

    Sx.dma("sp", dkT[:], DKT, (), ["dkT"])
    Sx.dma("sp", ikT[:], IKT, (), ["ikT"])
    Sx.memset("pool", dvaug[:, :, 64:128], 1.0, ["dvaug1"])
    Sx.dma("sp", dvaug[:, :, 0:64], DV.rearrange("(kb p) c -> p kb c", p=128), (), ["dvaug"])
    Sx.memset("dve", NEGD[:], 0.0, ["NEGD"])
    Sx.memset("dve", NEGD[0:64, 64:128], -BIG, ["NEGD"])
    for it in range(NIT):
        Sx.memset("dve", pow2[:, it:it + 1], 2.0 ** (-it), ["pow2"])
    IQTv = IQT.rearrange("(h d) t -> d h t", d=32)
    DQTv = DQT.rearrange("(h d) t -> d h t", d=64)
    YBTv = YBT.rearrange("(h d) t -> d h t", d=64)

    def stage1(i):
        nk = 128 * (i + 1)
        iq_ = iqTi[i % 2]
        Sx.dma("sp", iq_[:], IQTv[:, :, i * 128:(i + 1) * 128], (), [("iqTi", i % 2)])
        n = 0
        c = 0
        for k0 in range(0, nk, 512):
            kn = min(512, nk - k0)
            s2 = sc2[c % 2]
            for h in range(8):
                db = n % 2
                tr_ = tmpc[n % 4]
                tres = ("tmpc", n % 4)
                n += 1
                Sx.mm(PS[db][:, 0:kn], iq_[:, h, :], ikT[:, k0:k0 + kn], True, True, [("iqTi", i % 2), "ikT"], [psb(db)])
                Sx.act(tr_[:, 0:kn], PS[db][:, 0:kn], AF.Relu, [psb(db), "LOHI"], [tres], scale=LO[:, i, h:h + 1])
                if h == 0:
                    Sx.ts("dve", score[:, k0:k0 + kn], tr_[:, 0:kn], HI[:, i, 0:1], 0.0, ALU.mult, ALU.add,
                          [tres, "LOHI"], [("score", k0)])
                elif h < NH_DVE:
                    Sx.stt("dve", score[:, k0:k0 + kn], tr_[:, 0:kn], HI[:, i, h:h + 1], score[:, k0:k0 + kn], ALU.mult, ALU.add,
                           [tres, ("score", k0), "LOHI"], [("score", k0)])
                elif h == NH_DVE:
                    Sx.act(s2[:, 0:kn], tr_[:, 0:kn], AF.Copy, [tres, "LOHI"], [("sc2", c % 2)], scale=HI[:, i, h:h + 1])
                else:
                    t2 = tmp2[h % 2]
                    Sx.act(t2[:, 0:kn], tr_[:, 0:kn], AF.Copy, [tres, "LOHI"], [("tmp2", h % 2)], scale=HI[:, i, h:h + 1])
                    Sx.tt("pool", s2[:, 0:kn], s2[:, 0:kn], t2[:, 0:kn], ALU.add, [("tmp2", h % 2), ("sc2", c % 2)], [("sc2", c % 2)])
            if NH_DVE < 8:
                Sx.tt("dve", score[:, k0:k0 + kn], score[:, k0:k0 + kn], s2[:, 0:kn], ALU.add,
                      [("score", k0), ("sc2", c % 2)], [("score", k0)])
            c += 1

    def stage2(i):
        nk = 128 * (i + 1)
        sres = [("score", k0) for k0 in range(0, nk, 512)]
        mb = maskb[i % 2]
        Sx.op("dve", lambda h: h.tensor_reduce(am[:], score[:, 0:nk], AX.X, ALU.max, apply_absolute_value=True), sres, ["am"])
        Sx.tt("dve", score[:, nk - 128:nk], score[:, nk - 128:nk], NEGD[:], ALU.add, sres + ["NEGD"], sres)
        Sx.ts("dve", am[:], am[:], 1.0, 0.0, ALU.add, ALU.add, ["am"], ["am"])
        Sx.ts("dve", step[:], pow2[:], am[:, 0:1], 0.0, ALU.mult, ALU.add, ["am", "pow2"], ["step"])
        Sx.memset("dve", tpt[:, 0:1], 0.0, ["tpt"])
        Sx.memset("dve", cnt[:], 0.0, ["cnt"])
        for it in range(NIT):
            Sx.ts("dve", mb[:, 0:nk], score[:, 0:nk], tpt[:, it:it + 1], 0.0, ALU.is_ge, ALU.add, sres + ["tpt"],
                  [("maskb", i % 2)], accum=cnt[:, it:it + 1])
            Sx.ts("dve", ge[:, it:it + 1], cnt[:, it:it + 1], float(TOPK) - 0.5, -0.5, ALU.is_ge, ALU.add, [("maskb", i % 2), "cnt"], ["ge"])
            Sx.stt("dve", tpt[:, it + 1:it + 2], ge[:, it:it + 1], step[:, it:it + 1], tpt[:, it:it + 1], ALU.mult, ALU.add,
                   ["ge", "step", "tpt"], ["tpt"])
        Sx.stt("dve", lo[:, 0:1], step[:, NIT - 1:NIT], -0.5, tpt[:, NIT:NIT + 1], ALU.mult, ALU.add, ["step", "tpt"], ["lo"])
        Sx.ts("dve", mb[:, 0:nk], score[:, 0:nk], lo[:, 0:1], 0.0, ALU.is_ge, ALU.add, sres + ["lo"], [("maskb", i % 2)])

    def stage3(i):
        nkb = i + 1
        mb = maskb[i % 2]
        dq_ = dqTi[i % 2]
        Sx.dma("sp", dq_[:].rearrange("p (h t) -> p h t", h=8), DQTv[:, :, i * 128:(i + 1) * 128], (), [("dqTi", i % 2)])
        for gi, g0 in enumerate(range(0, nkb, 4)):
            gn = min(4, nkb - g0)
            tbk = gi % 2
            pb = ps_bf(tbk)
            for kk in range(gn):
                kb = g0 + kk
                Sx.tr(pb[:, kk * 128:(kk + 1) * 128], mb[:, kb * 128:(kb + 1) * 128], ident[:], [("maskb", i % 2), "ident"], [psb(tbk)])
            Sx.cp("act", maskT[:, g0:g0 + gn, :], pb[:, 0:gn * 128].rearrange("p (c t) -> p c t", c=gn), [psb(tbk)], [("maskT", g0 // 4)])

        def qk(kb):
            for half in range(2):
                sb_ = 2 + (2 * kb + half) % 4
                Sx.mm(PS[sb_][:], dkT[:, kb * 128:(kb + 1) * 128], dq_[:, half * 512:(half + 1) * 512], True, True,
                      ["dkT", ("dqTi", i % 2)], [psb(sb_)])

        def rest(kb):
            sl = kb % 2
            for half in range(2):
                sb_ = 2 + (2 * kb + half) % 4
                Sx.act(pTd[sl][:, half * 512:(half + 1) * 512], PS[sb_][:], AF.Exp, [psb(sb_)], [("pTd", sl, half)], scale=0.125)
            Sx.tt("pool", pTd[sl][:].rearrange("p (h t) -> p h t", h=8), pTd[sl][:].rearrange("p (h t) -> p h t", h=8),
                  maskT[:, kb, :].unsqueeze(1).to_broadcast([128, 8, 128]), ALU.mult,
                  [("pTd", sl, 0), ("pTd", sl, 1), ("maskT", kb // 4)], [("pTd", sl, 0), ("pTd", sl, 1)])
            for half in range(2):
                Sx.mm(PS[6 + half][:], dvaug[:, kb, :], pTd[sl][:, half * 512:(half + 1) * 512], kb == 0, kb == nkb - 1,
                      [("pTd", sl, half), "dvaug", "dvaug1"], [psb(6 + half)])

        qk(0)
        for kb in range(nkb):
            if kb + 1 < nkb:
                qk(kb + 1)
            rest(kb)
        for half in range(2):
            Sx.recip(rc2[64:128, :], PS[6 + half][64:128, :], [psb(6 + half)], ["rc2"])
            Sx.tt("dve", ybo[i % 2][0:64, half * 512:(half + 1) * 512], PS[6 + half][0:64, :], rc2[64:128, :], ALU.mult,
                  [psb(6 + half), "rc2"], [("ybo", i % 2)])
        Sx.dma("pool", YBTv[:, :, i * 128:(i + 1) * 128], ybo[i % 2][0:64, :].rearrange("p (h t) -> p h t", h=8),
               [("ybo", i % 2)], [("YBT", i)])

    stage1(0)
    stage2(0)
    for i in range(NT):
        if i + 1 < NT:
            stage1(i + 1)
            stage2(i + 1)
        stage3(i)
    Sx.barrier()
    al.release(mD)

    Sx.enabled = "C" in phases
    ring2 = Ring("wr2", 6, 4096)
    xT_sb = al("xT_c", [128, KC, 512], F32)
    hT_sb = al("hT_c", [128, KC, 512], BF16)
    sq = [al("sqc%d" % i, [128, 512], F32) for i in range(2)]
    tmp = [al("tmpc_%d" % i, [128, 512], F32) for i in range(2)]
    rs = al("rsc", [128, 512], F32)
    rstd = al("rstdc", [128, 512], F32)
    aT = al("aTc", [128, NJ, 512], BF16)
    sil = [al("silc%d" % i, [128, 512], F32) for i in range(2)]
    sgc = al("sgc", [128, 16, 512], BF16)
    yaT = al("yaT", [128, 4, 512], BF16)
    ybT = al("ybT", [128, 4, 512], BF16)
    wbf_sb = al("wbf_sb", [128, 4, D], BF16)
    wbd_sb = al("wbd_sb", [128, 4, D], BF16)
    wo_sb = al("wo_sb", [128, KC, D], BF16)
    mg = al("mg", [128, KC, 512], BF16)
    m1 = [al("m1_%d" % i, [128, 512], F32) for i in range(2)]
    m2 = [al("m2_%d" % i, [128, 512], F32) for i in range(2)]
    Sx.dma("sp", wbf_sb[:].rearrange("p a b -> p (a b)"), wb["wbf"], w_ready("wbf"), ["wbf"])
    Sx.dma("sp", wbd_sb[:].rearrange("p a b -> p (a b)"), wb["wbd"], w_ready("wbd"), ["wbd"])
    Sx.dma("sp", wo_sb[:].rearrange("p a b -> p (a b)"), wb["wo"], w_ready("wo"), ["wo"])
    YATv = YAT.rearrange("(c p) t -> p c t", p=128)
    YBTv2 = YBT.rearrange("(c p) t -> p c t", p=128)
    for tb in range(NB):
        t0 = tb * 512
        Sx.dma("sp", xT_sb[:], X1Tv[:, :, t0:t0 + 512], (), xres)
        Sx.dma("sp", yaT[:], YATv[:, :, t0:t0 + 512], (), ["yaT"])
        Sx.dma("sp", ybT[:], YBTv2[:, :, t0:t0 + 512], (), ["ybT"])
        Sx.dma("sp", sgc[:], SGTv[:, :, t0:t0 + 512], (), ["sgc"])
        for m in range(KC):
            for c in range(4):
                Sx.mm(PS[0][:], wbf_sb[:, c, m * 128:(m + 1) * 128], yaT[:, c, :], c == 0, c == 3, ["wbf", "yaT"], [psb(0)])
            for c in range(4):
                Sx.mm(PS[1][:], wbd_sb[:, c, m * 128:(m + 1) * 128], ybT[:, c, :], c == 0, c == 3, ["wbd", "ybT"], [psb(1)])
            Sx.tt("dve", m1[m % 2][:], PS[0][:], sgc[:, m, :], ALU.mult, [psb(0), "sgc"], [("m1", m % 2)])
            Sx.tt("dve", m2[m % 2][:], PS[1][:], sgc[:, 8 + m, :], ALU.mult, [psb(1), "sgc"], [("m2", m % 2)])
            Sx.tt("pool", mg[:, m, :], m1[m % 2][:], m2[m % 2][:], ALU.add, [("m1", m % 2), ("m2", m % 2)], [("mg", m)])
        for m in range(KC):
            yb_ = 4 + (m % 2)
            for kc in range(KC):
                Sx.mm(PS[yb_][:], wo_sb[:, kc, m * 128:(m + 1) * 128], mg[:, kc, :], kc == 0, kc == KC - 1, ["wo", ("mg", kc)], [psb(yb_)])
            Sx.stt("dve", xT_sb[:, m, :], PS[yb_][:], coefG[:, 1, m:m + 1], xT_sb[:, m, :], ALU.mult, ALU.add,
                   [psb(yb_), ("xT", m), "coef"], [("xT", m)])
        norm_mod(2, xT_sb, hT_sb, sq, tmp, rs, rstd)
        ffn(2, "f2", ring2, xT_sb, hT_sb, aT, sil)
        Sx.dma("pool", outTv[:, :, t0:t0 + 512], xT_sb[:], xres, [("out", tb)])
    Sx.enabled = True
    Sx.barrier()

    with nc.Block() as block:
        Sx.emit(block)
    es.close()
    return nc


def _prep_shared(inp):
    f = np.float32
    d = {}
    ada_w = np.asarray(inp["ada_w"], f)[0]
    d["ada_w"] = np.ascontiguousarray(ada_w.reshape(KC, 128, 9 * D).transpose(1, 0, 2))
    d["ada_bT"] = np.ascontiguousarray(np.asarray(inp["ada_b"], f)[0].reshape(72, 128).T)
    d["norm_gT"] = np.ascontiguousarray(np.asarray(inp["norm_g"], f)[0].reshape(3, KC, 128).transpose(2, 0, 1))
    for p, name in (("f1", "ffn1"), ("f2", "ffn2")):
        w1 = np.asarray(inp[name + "_w1"], f)[0]
        w3 = np.asarray(inp[name + "_w3"], f)[0]
        w2 = np.asarray(inp[name + "_w2"], f)[0]
        d[p + "w1"] = np.ascontiguousarray(w1.reshape(KC, 128, NJ, 128).transpose(1, 2, 0, 3)).reshape(128, -1)
        d[p + "w3"] = np.ascontiguousarray(w3.reshape(KC, 128, NJ, 128).transpose(1, 2, 0, 3)).reshape(128, -1)
        d[p + "w2"] = np.ascontiguousarray(w2.reshape(NJ, 128, KC, 128).transpose(1, 2, 0, 3)).reshape(128, -1)
    w_in = np.asarray(inp["w_in"], f)[0]
    offs = dict(fq=0, fk=512, fv=1024, ff=1536, dq=1544, dk=2056, dv=2120, iq=2184, ik=2440, iw=2472, ga=2480, gb=3504)
    g4 = np.concatenate([w_in[:, offs["ff"]:offs["ff"] + 8], w_in[:, offs["dk"]:offs["dk"] + 64],
                         w_in[:, offs["dv"]:offs["dv"] + 64], w_in[:, offs["iq"]:offs["iq"] + 256],
                         w_in[:, offs["ik"]:offs["ik"] + 32], w_in[:, offs["iw"]:offs["iw"] + 8],
                         np.zeros((D, 80), f)], axis=1)
    groups = [w_in[:, 0:512], w_in[:, 512:1024], w_in[:, 1024:1536], w_in[:, offs["dq"]:offs["dq"] + 512], g4]
    wt = np.stack(groups, 0)
    d["wt"] = np.ascontiguousarray(wt.reshape(5, KC, 128, 512).transpose(2, 0, 1, 3)).reshape(128, -1)
    wg = w_in[:, 2480:4528]
    d["wg"] = np.ascontiguousarray(wg.reshape(KC, 128, 16, 128).transpose(1, 2, 0, 3)).reshape(128, -1)
    d["wbf"] = np.ascontiguousarray(np.asarray(inp["w_br_fox"], f)[0].reshape(4, 128, D).transpose(1, 0, 2)).reshape(128, -1)
    d["wbd"] = np.ascontiguousarray(np.asarray(inp["w_br_dsa"], f)[0].reshape(4, 128, D).transpose(1, 0, 2)).reshape(128, -1)
    d["wo"] = np.ascontiguousarray(np.asarray(inp["w_out"], f)[0].reshape(KC, 128, D).transpose(1, 0, 2)).reshape(128, -1)
    d["fbias"] = np.ascontiguousarray(np.asarray(inp["fox_f_bias"], f)[0].reshape(1, 8))
    d["fqg"] = np.ascontiguousarray(np.asarray(inp["fox_qk_g"], f)[0].reshape(1, 128))
    d["dqg"] = np.ascontiguousarray(np.asarray(inp["dsa_qk_g"], f)[0].reshape(1, 128))
    return d


def _prep_core(inp, b, S):
    d = {}
    d["xT"] = np.ascontiguousarray(np.asarray(inp["x"], np.float32)[b, :S].T)
    d["cT"] = np.ascontiguousarray(np.asarray(inp["c"], np.float32)[b].reshape(KC, 128).T)
    d["pos"] = np.ascontiguousarray(np.asarray(inp["positions"], np.int32)[b, :S].reshape(S // 128, 128).T)
    return d


_NC_CACHE = {}


def kernel(**inputs):
    B, S, _ = inputs["x"].shape
    if S not in _NC_CACHE:
        _NC_CACHE[S] = build(S)
    nc = _NC_CACHE[S]
    shared = _prep_shared(inputs)
    in_maps = []
    for b in range(B):
        m = dict(shared)
        m.update(_prep_core(inputs, b, S))
        in_maps.append(m)
    res = run_bass_kernel_spmd(nc, in_maps, core_ids=list(range(B)))
    out = np.stack([np.ascontiguousarray(res.results[b]["outT"].T) for b in range(B)], 0)
    return out.astype(np.float32)
```
